# Optimizing a Trainium2 kernel written in Bass

```python
import jax, jax.numpy as jnp
from jax import lax
import numpy as np

D_MODEL = 1024
BATCH = 8
SEQ = 4096
DEPTH = 2

N_A_LAYERS = DEPTH // 2
N_B_LAYERS = DEPTH - N_A_LAYERS

HEAD_DIM = 128
A_HEADS = 6
A_WIDTH = A_HEADS * HEAD_DIM
CHUNK = 64

B_HEADS = 6
B_WIDTH = B_HEADS * HEAD_DIM
DILATED_GROUPS = ((128, 1), (512, 4), (2048, 16))
N_GROUPS = len(DILATED_GROUPS)
ROPE_THETA = 10000.0

MEM_TOKENS = 256
MEM_HEADS = 4
MEM_HEAD_DIM = 64
MEM_WIDTH = MEM_HEADS * MEM_HEAD_DIM

MIX_WIDTH = A_WIDTH + MEM_WIDTH
A_COLS = 4 * A_WIDTH + MEM_WIDTH
B_COLS = N_GROUPS * B_WIDTH + MEM_WIDTH
FFN_HIDDEN = ((-(-8 * D_MODEL // 3)) + 255) // 256 * 256
EPS = 1e-6

kernel_name = "yoco_hgrn2_dilated_attn_memory_hybrid"


def rms_norm(x, g):
    xf = x.astype(jnp.float32)
    y = xf * lax.rsqrt(jnp.mean(xf * xf, axis=-1, keepdims=True) + EPS)
    return (y * g.astype(jnp.float32)).astype(x.dtype)


def rope(x, pos):
    dh = x.shape[-1]
    half = dh // 2
    inv = ROPE_THETA ** (-jnp.arange(half, dtype=jnp.float32) / half)
    ang = pos.astype(jnp.float32)[:, None] * inv[None, :]
    cos = jnp.cos(ang)[None, :, None, :]
    sin = jnp.sin(ang)[None, :, None, :]
    xf = x.astype(jnp.float32)
    x1, x2 = xf[..., :half], xf[..., half:]
    out = jnp.concatenate([x1 * cos - x2 * sin, x2 * cos + x1 * sin], axis=-1)
    return out.astype(x.dtype)


def hgrn2_chunkwise(q, f_logit, i, lb):
    Bn, S, H, dk = q.shape
    nC = S // CHUNK
    f = lb + (1.0 - lb) * jax.nn.sigmoid(f_logit.astype(jnp.float32))
    k = 1.0 - f
    logf = jnp.log(f)
    qf = jax.nn.silu(q.astype(jnp.float32))
    vf = i.astype(jnp.float32)

    def chunks(t):
        return t.reshape(Bn, nC, CHUNK, H, t.shape[-1]).transpose(1, 0, 3, 2, 4)

    qc, kc, vc, gc = chunks(qf), chunks(k), chunks(vf), chunks(logf)
    b = jnp.cumsum(gc, axis=3)
    b_end = b[:, :, :, -1:, :]
    q_in = qc * jnp.exp(b)
    k_in = kc * jnp.exp(-b)
    k_out = kc * jnp.exp(b_end - b)
    causal = jnp.tril(jnp.ones((CHUNK, CHUNK), dtype=bool))
    att = jnp.where(causal, jnp.einsum('nbhqd,nbhkd->nbhqk', q_in, k_in), 0.0)
    o_intra = jnp.einsum('nbhqk,nbhke->nbhqe', att, vc)
    decay = jnp.exp(b_end[:, :, :, 0, :])

    def step(state, xs):
        q_n, k_n, v_n, dec = xs
        o_n = jnp.einsum('bhqd,bhde->bhqe', q_n, state)
        state = dec[..., None] * state + jnp.einsum('bhkd,bhke->bhde', k_n, v_n)
        return state, o_n

    s0 = jnp.zeros((Bn, H, dk, vf.shape[-1]), jnp.float32)
    _, o_inter = lax.scan(step, s0, (q_in, k_out, vc, decay))
    o = o_intra + o_inter
    return o.transpose(1, 0, 3, 2, 4).reshape(Bn, S, H, vf.shape[-1])


def dilated_branch(q, k, v, window, dilation):
    Bn, S, H, dh = q.shape
    span = window // dilation
    blk = span
    L = S // dilation
    nb = -(-L // blk)
    Lp = nb * blk

    def by_residue(t):
        t = t.reshape(Bn, L, dilation, H, dh).transpose(0, 2, 3, 1, 4)
        return jnp.pad(t, ((0, 0), (0, 0), (0, 0), (0, Lp - L), (0, 0)))

    def band(t):
        tp = jnp.pad(t, ((0, 0), (0, 0), (0, 0), (blk, 0), (0, 0)))
        prev = tp[:, :, :, :Lp].reshape(Bn, dilation, H, nb, blk, dh)
        cur = tp[:, :, :, blk:].reshape(Bn, dilation, H, nb, blk, dh)
        return jnp.concatenate([prev, cur], axis=4)

    qb = by_residue(q).reshape(Bn, dilation, H, nb, blk, dh)
    kb = band(by_residue(k))
    vb = band(by_residue(v))
    s = jnp.einsum('brhnqd,brhnkd->brhnqk', qb, kb,
                   preferred_element_type=jnp.float32) * (dh ** -0.5)
    qpos = jnp.arange(nb)[:, None, None] * blk + jnp.arange(blk)[None, :, None]
    kpos = (jnp.arange(nb)[:, None, None] - 1) * blk + jnp.arange(2 * blk)[None, None, :]
    dist = qpos - kpos
    mask = (dist >= 0) & (dist <= span) & (kpos >= 0)
    s = jnp.where(mask, s, -jnp.inf)
    lse = jax.nn.logsumexp(s, axis=-1)
    p = jnp.exp(s - lse[..., None])
    o = jnp.einsum('brhnqk,brhnkd->brhnqd', p.astype(v.dtype), vb)
    o = o.reshape(Bn, dilation, H, Lp, dh)[:, :, :, :L].transpose(0, 3, 1, 2, 4).reshape(Bn, S, H, dh)
    lse = lse.reshape(Bn, dilation, H, Lp)[..., :L].transpose(0, 3, 1, 2).reshape(Bn, S, H)
    return o, lse


def memory_attention(q, mk, mv):
    s = jnp.einsum('bshd,bmhd->bhsm', q, mk,
                   preferred_element_type=jnp.float32) * (q.shape[-1] ** -0.5)
    p = jax.nn.softmax(s, axis=-1)
    return jnp.einsum('bhsm,bmhd->bshd', p.astype(mv.dtype), mv)


def setup_inputs(seed: int = 0) -> dict:
    key = jax.random.key(seed)
    ks = jax.random.split(key, 24)

    def w(k, shape, fan_in):
        return jax.random.normal(k, shape, jnp.float32) * (fan_in ** -0.5)

    def gain(k, shape):
        return 1.0 + 0.02 * jax.random.normal(k, shape, jnp.float32)

    return {
        "x": jax.random.normal(ks[0], (BATCH, SEQ, D_MODEL), jnp.float32),
        "mem": jax.random.normal(ks[1], (BATCH, MEM_TOKENS, D_MODEL), jnp.float32),
        "norm_mix": gain(ks[2], (DEPTH, D_MODEL)),
        "norm_ffn": gain(ks[3], (DEPTH, D_MODEL)),
        "a_w_in": w(ks[4], (N_A_LAYERS, D_MODEL, A_COLS), D_MODEL),
        "a_lb_logits": 0.1 * jax.random.normal(ks[5], (N_A_LAYERS + 1, A_WIDTH), jnp.float32),
        "a_onorm": gain(ks[6], (N_A_LAYERS, A_WIDTH)),
        "b_w_in": w(ks[7], (N_B_LAYERS, D_MODEL, B_COLS), D_MODEL),
        "b_qnorm": gain(ks[8], (N_B_LAYERS, N_GROUPS, HEAD_DIM)),
        "kv_norm": gain(ks[9], (D_MODEL,)),
        "w_kv": w(ks[10], (D_MODEL, 2 * B_WIDTH), D_MODEL),
        "b_knorm": gain(ks[11], (HEAD_DIM,)),
        "mem_norm": gain(ks[12], (DEPTH, D_MODEL)),
        "w_mem_kv": w(ks[13], (DEPTH, D_MODEL, 2 * MEM_WIDTH), D_MODEL),
        "mem_qnorm": gain(ks[14], (DEPTH, MEM_HEAD_DIM)),
        "mem_knorm": gain(ks[15], (DEPTH, MEM_HEAD_DIM)),
        "w_out": w(ks[16], (DEPTH, MIX_WIDTH, D_MODEL), MIX_WIDTH),
        "w_gate_up": w(ks[17], (DEPTH, D_MODEL, 2 * FFN_HIDDEN), D_MODEL),
        "w_down": w(ks[18], (DEPTH, FFN_HIDDEN, D_MODEL), FFN_HIDDEN),
    }


def reference(x, mem, norm_mix, norm_ffn, a_w_in, a_lb_logits, a_onorm, b_w_in, b_qnorm,
              kv_norm, w_kv, b_knorm, mem_norm, w_mem_kv, mem_qnorm, mem_knorm,
              w_out, w_gate_up, w_down):
    Bn, S, _ = x.shape
    pos = jnp.arange(S)
    lb_all = jnp.cumsum(jax.nn.softmax(a_lb_logits.astype(jnp.float32), axis=0), axis=0)
    h = x
    k_sh = None
    v_sh = None
    for l in range(DEPTH):
        xn = rms_norm(h, norm_mix[l])
        mn = rms_norm(mem, mem_norm[l])
        mk, mv = jnp.split(mn @ w_mem_kv[l], 2, axis=-1)
        mk = rms_norm(mk.reshape(Bn, MEM_TOKENS, MEM_HEADS, MEM_HEAD_DIM), mem_knorm[l])
        mv = mv.reshape(Bn, MEM_TOKENS, MEM_HEADS, MEM_HEAD_DIM)
        if l < N_A_LAYERS:
            proj = xn @ a_w_in[l]
            q, f, i, g, mq = jnp.split(proj, [A_WIDTH, 2 * A_WIDTH, 3 * A_WIDTH, 4 * A_WIDTH], axis=-1)
            shp = (Bn, S, A_HEADS, HEAD_DIM)
            o = hgrn2_chunkwise(q.reshape(shp), f.reshape(shp), i.reshape(shp),
                                lb_all[l].reshape(A_HEADS, HEAD_DIM))
            o = rms_norm(o, a_onorm[l].reshape(A_HEADS, HEAD_DIM)) * jax.nn.silu(g.reshape(shp).astype(jnp.float32))
            mix_main = o.reshape(Bn, S, A_WIDTH).astype(h.dtype)
        else:
            j = l - N_A_LAYERS
            proj = xn @ b_w_in[j]
            qs = proj[..., :N_GROUPS * B_WIDTH].reshape(Bn, S, N_GROUPS, B_HEADS, HEAD_DIM)
            mq = proj[..., N_GROUPS * B_WIDTH:]
            outs = []
            lses = []
            for gi, (win, dil) in enumerate(DILATED_GROUPS):
                qg = rope(rms_norm(qs[:, :, gi], b_qnorm[j, gi]), pos)
                o_g, lse_g = dilated_branch(qg, k_sh, v_sh, win, dil)
                outs.append(o_g)
                lses.append(lse_g)
            alpha = jax.nn.softmax(jnp.stack(lses, axis=0), axis=0)
            o = jnp.sum(alpha[..., None] * jnp.stack(outs, axis=0).astype(jnp.float32), axis=0)
            mix_main = o.reshape(Bn, S, B_WIDTH).astype(h.dtype)
        mq = rms_norm(mq.reshape(Bn, S, MEM_HEADS, MEM_HEAD_DIM), mem_qnorm[l])
        mo = memory_attention(mq, mk, mv).reshape(Bn, S, MEM_WIDTH)
        h = h + jnp.concatenate([mix_main, mo.astype(h.dtype)], axis=-1) @ w_out[l]
        gt, up = jnp.split(rms_norm(h, norm_ffn[l]) @ w_gate_up[l], 2, axis=-1)
        h = h + (jax.nn.silu(gt) * up) @ w_down[l]
        if l == N_A_LAYERS - 1:
            k_sh, v_sh = jnp.split(rms_norm(h, kv_norm) @ w_kv, 2, axis=-1)
            k_sh = rope(rms_norm(k_sh.reshape(Bn, S, B_HEADS, HEAD_DIM), b_knorm), pos)
            v_sh = v_sh.reshape(Bn, S, B_HEADS, HEAD_DIM)
    return h
```

```python
from contextlib import ExitStack
import numpy as np
import ml_dtypes
import concourse.bass as bass
import concourse.mybir as mybir
from concourse.bass_utils import run_bass_kernel_spmd

F32 = mybir.dt.float32
BF16 = mybir.dt.bfloat16
AF = mybir.ActivationFunctionType
ALU = mybir.AluOpType

COMPUTE = ("pe", "act", "dve", "pool")

D = 1024
S = 4096
T = 512
NT = S // T
NMEM = 256
FFN = 2816
NHC = FFN // 128
EPS = 1e-6
DILS = (1, 4, 16)
NSLOT = 5
NWB = 3


class Tile:
    def __init__(self, h, name):
        self.h = h
        self.name = name
        self.w_by = {}
        self.r_by = {}
        self.sems = {}
        self.sem_count = {}

    def __getitem__(self, idx):
        return self.h[idx]


class Op:
    __slots__ = ("eng", "fn", "deps", "needs_inc", "tick", "is_dma", "sem_tile", "dma_count", "key", "qc")

    def __init__(self, eng, fn):
        self.eng = eng
        self.fn = fn
        self.deps = []
        self.needs_inc = False
        self.tick = None
        self.is_dma = False
        self.sem_tile = None
        self.dma_count = None
        self.key = eng
        self.qc = None


class _Rec:
    def __init__(self):
        self.calls = []

    def __getattr__(self, name):
        def f(*a, **k):
            self.calls.append((name, a, k))
            return self
        return f


class Prog:
    def __init__(self, nc):
        self.nc = nc
        self.es = ExitStack()
        self.ops = {e: [] for e in ("pe", "act", "dve", "pool", "sp")}
        self.tiles = []

    def sbuf(self, name, shape, dtype):
        h = self.es.enter_context(self.nc.sbuf_tensor(name, list(shape), dtype))
        t = Tile(h, name)
        self.tiles.append(t)
        return t

    def psum(self, name, shape, dtype=F32):
        h = self.es.enter_context(self.nc.psum_tensor(name, list(shape), dtype))
        t = Tile(h, name)
        self.tiles.append(t)
        return t

    def dram(self, name, shape, dtype, kind="Internal"):
        h = self.nc.dram_tensor(name, list(shape), dtype, kind=kind)
        t = Tile(h, name)
        self.tiles.append(t)
        return t

    def _track(self, op, reads, writes):
        deps = {}
        for t in reads:
            for w in t.w_by.values():
                deps[id(w)] = w
        for t in writes:
            for w in t.w_by.values():
                deps[id(w)] = w
            for r in t.r_by.values():
                deps[id(r)] = r
        for d in deps.values():
            if d is op:
                continue
            if d.eng == "pe" and op.eng == "pe":
                continue
            op.deps.append(d)
            d.needs_inc = True
        for t in reads:
            t.r_by[op.key] = op
        for t in writes:
            t.w_by[op.key] = op

    def add(self, eng, fn, reads=(), writes=()):
        rec = _Rec()
        fn(rec)
        assert len(rec.calls) == 1
        name, a, k = rec.calls[0]
        op = Op(eng, lambda e: getattr(e, name)(*a, **k))
        self._track(op, reads, writes)
        self.ops[eng].append(op)
        return op

    def dma(self, queue, out_ap, in_ap, reads=(), writes=(), sem_tile=None, **kw):
        if sem_tile is None:
            sem_tile = writes[0]
        op = Op(queue, lambda e: e.dma_start(out=out_ap, in_=in_ap, **kw))
        op.is_dma = True
        op.sem_tile = sem_tile
        qc = "sw" if queue == "pool" else "hw"
        op.qc = qc
        op.key = ("dma", id(sem_tile), qc)
        sem_tile.sem_count[qc] = sem_tile.sem_count.get(qc, 0) + 1
        op.dma_count = sem_tile.sem_count[qc]
        op.needs_inc = True
        self._track(op, reads, writes)
        self.ops[queue].append(op)
        return op

    def emit(self, final_wait_tiles=()):
        nc = self.nc
        es = self.es
        esem = {e: es.enter_context(nc.semaphore("sem_" + e)) for e in COMPUTE}
        for t in self.tiles:
            for qc in t.sem_count:
                t.sems[qc] = es.enter_context(nc.semaphore("dsem_%s_%s" % (qc, t.name)))
        for e in COMPUTE:
            c = 0
            for op in self.ops[e]:
                if op.is_dma:
                    continue
                if op.needs_inc:
                    c += 1
                    op.tick = c
        block = es.enter_context(nc.Block())

        def target(d):
            if d.is_dma:
                return d.sem_tile.sems[d.qc], 16 * d.dma_count
            return esem[d.eng], d.tick

        def run(ename, eng):
            waited = {}
            for op in self.ops[ename]:
                need = {}
                for d in op.deps:
                    s, v = target(d)
                    k = id(s)
                    if waited.get(k, 0) >= v:
                        continue
                    if k not in need or need[k][1] < v:
                        need[k] = (s, v)
                for k, (s, v) in need.items():
                    eng.wait_ge(s, v)
                    waited[k] = v
                ins = op.fn(eng)
                if op.is_dma:
                    ins.then_inc(op.sem_tile.sems[op.qc], 16)
                elif op.needs_inc:
                    ins.then_inc(esem[op.eng], 1)
            if ename == "sp":
                for t in final_wait_tiles:
                    for qc, sm in t.sems.items():
                        eng.wait_ge(sm, 16 * t.sem_count[qc])

        @block.sync
        def _(e):
            run("sp", e)

        @block.tensor
        def _(e):
            run("pe", e)

        @block.scalar
        def _(e):
            run("act", e)

        @block.vector
        def _(e):
            run("dve", e)

        @block.gpsimd
        def _(e):
            run("pool", e)

    def close(self):
        self.es.close()


def _grp(wsub):
    K, n = wsub.shape
    kc = K // 128
    return np.ascontiguousarray(wsub.reshape(kc, 128, n).transpose(1, 0, 2).reshape(128, kc * n))


def weight_groups(a_w_in, b_w_in, w_kv, w_mem_kv, w_out, w_gate_up, w_down):
    g = {}
    A = a_w_in[0]
    for h in range(6):
        cols = np.concatenate([np.arange(k * 768 + h * 128, k * 768 + (h + 1) * 128) for k in range(4)])
        g["a%d" % h] = (_grp(A[:, cols]), 8, 512)
    g["amq"] = (_grp(A[:, 3072:3328]), 8, 256)
    Bw = b_w_in[0]
    for h in range(6):
        cols = np.concatenate([np.arange(k * 768 + h * 128, k * 768 + (h + 1) * 128) for k in range(3)])
        g["b%d" % h] = (_grp(Bw[:, cols]), 8, 384)
    g["bmq"] = (_grp(Bw[:, 2304:2560]), 8, 256)
    g["kv0"] = (_grp(w_kv[:, 0:512]), 8, 512)
    g["kv1"] = (_grp(w_kv[:, 512:1024]), 8, 512)
    g["kv2"] = (_grp(w_kv[:, 1024:1536]), 8, 512)
    for l in range(2):
        g["mkv%d" % l] = (_grp(w_mem_kv[l]), 8, 512)
        g["wo%d_0" % l] = (_grp(w_out[l][:, 0:512]), 8, 512)
        g["wo%d_1" % l] = (_grp(w_out[l][:, 512:1024]), 8, 512)
        for i in range(NHC // 2):
            cols = np.concatenate([np.arange(hc * 128, (hc + 1) * 128) + off
                                   for hc in (2 * i, 2 * i + 1) for off in (0, FFN)])
            g["gu%d_%d" % (l, i)] = (_grp(w_gate_up[l][:, cols]), 8, 512)
        for j in range(8):
            g["wd%d_%d" % (l, j)] = (_grp(w_down[l][:, j * 128:(j + 1) * 128]), NHC, 128)
    return g


GROUP_SHAPES = {}
for _h in range(6):
    GROUP_SHAPES["a%d" % _h] = (8, 512)
    GROUP_SHAPES["b%d" % _h] = (8, 384)
GROUP_SHAPES["amq"] = (8, 256)
GROUP_SHAPES["bmq"] = (8, 256)
for _n in ("kv0", "kv1", "kv2"):
    GROUP_SHAPES[_n] = (8, 512)
for _l in range(2):
    GROUP_SHAPES["mkv%d" % _l] = (8, 512)
    GROUP_SHAPES["wo%d_0" % _l] = (8, 512)
    GROUP_SHAPES["wo%d_1" % _l] = (8, 512)
    for _i in range(NHC // 2):
        GROUP_SHAPES["gu%d_%d" % (_l, _i)] = (8, 512)
    for _j in range(8):
        GROUP_SHAPES["wd%d_%d" % (_l, _j)] = (NHC, 128)


def host_consts():
    c = {}
    c["ones_d"] = np.full((128, 128), 1.0 / D, ml_dtypes.bfloat16)
    c["ones_h"] = np.full((128, 128), 1.0 / 128, ml_dtypes.bfloat16)
    o64 = np.zeros((128, 128), np.float32)
    o64[:64, :64] = 1.0 / 64
    o64[64:, 64:] = 1.0 / 64
    c["ones_64"] = o64.astype(ml_dtypes.bfloat16)
    c["ones_f"] = np.ones((128, 128), ml_dtypes.bfloat16)
    bl = np.zeros((128, 2, 128), np.float32)
    bl[:, 0, :64] = 1.0
    bl[:, 1, 64:] = 1.0
    c["blk_ones"] = bl.astype(ml_dtypes.bfloat16)
    c["ident"] = np.eye(128, dtype=np.float32).astype(ml_dtypes.bfloat16)
    R = np.zeros((128, 128), np.float32)
    for do in range(64):
        R[do + 64, do] = -1.0
    for do in range(64, 128):
        R[do - 64, do] = 1.0
    c["rrot"] = R.astype(ml_dtypes.bfloat16)
    kk = np.arange(128)[:, None]
    qq = np.arange(128)[None, :]
    cm = ((kk <= qq) & (kk // 64 == qq // 64)).astype(np.float32)
    c["cmask"] = np.ascontiguousarray(np.broadcast_to(cm[:, None, :], (128, 4, 128))).astype(ml_dtypes.bfloat16)
    rm = np.ones((128, T), np.float32)
    rm[:, ::64] = 0.0
    c["rmask"] = rm
    for dil in DILS:
        W = 128 * dil + 128
        x = np.arange(W)[None, :]
        dd = x - kk
        m = ((dd >= 0) & (dd <= 128 * dil) & (dd % dil == 0)).astype(np.float32)
        c["dmask%d" % dil] = m.astype(ml_dtypes.bfloat16)
    half = 64
    inv = (10000.0 ** (-np.arange(half, dtype=np.float32) / half)).astype(np.float32)
    ang = np.arange(S, dtype=np.float32)[:, None] * inv[None, :]
    cos = np.cos(ang).astype(np.float32).T
    sin = np.sin(ang).astype(np.float32).T
    c["cosT"] = np.ascontiguousarray(np.concatenate([cos, cos], 0))
    c["sinT"] = np.ascontiguousarray(np.concatenate([sin, sin], 0))
    return c


CONST_SHAPES = {
    "ones_d": ([128, 128], BF16), "ones_h": ([128, 128], BF16), "ones_64": ([128, 128], BF16),
    "ones_f": ([128, 128], BF16), "blk_ones": ([128, 2, 128], BF16), "ident": ([128, 128], BF16),
    "rrot": ([128, 128], BF16), "cmask": ([128, 4, 128], BF16), "rmask": ([128, T], F32),
    "dmask1": ([128, 256], BF16), "dmask4": ([128, 640], BF16), "dmask16": ([128, 2176], BF16),
}

PC = {}
_o = 0
for _name, _n in (("norm_mix", 16), ("norm_ffn", 16), ("kv_norm", 8), ("mem_norm", 16), ("a_onorm", 6),
                  ("b_qnorm", 3), ("b_knorm", 1), ("mem_qnorm", 2), ("mem_knorm", 2), ("lb0", 6), ("lb1", 6)):
    PC[_name] = _o
    _o += _n
NPC = _o


def host_params(norm_mix, norm_ffn, a_lb_logits, a_onorm, b_qnorm, kv_norm, b_knorm, mem_norm,
                mem_qnorm, mem_knorm):
    p = np.zeros((128, NPC), np.float32)

    def colmaj(v):
        return v.reshape(-1, 128).T

    for l in range(2):
        p[:, PC["norm_mix"] + 8 * l:PC["norm_mix"] + 8 * l + 8] = colmaj(norm_mix[l])
        p[:, PC["norm_ffn"] + 8 * l:PC["norm_ffn"] + 8 * l + 8] = colmaj(norm_ffn[l])
        p[:, PC["mem_norm"] + 8 * l:PC["mem_norm"] + 8 * l + 8] = colmaj(mem_norm[l])
        p[:, PC["mem_qnorm"] + l] = np.tile(mem_qnorm[l], 2)
        p[:, PC["mem_knorm"] + l] = np.tile(mem_knorm[l], 2)
    p[:, PC["kv_norm"]:PC["kv_norm"] + 8] = colmaj(kv_norm)
    p[:, PC["a_onorm"]:PC["a_onorm"] + 6] = colmaj(a_onorm[0])
    p[:, PC["b_qnorm"]:PC["b_qnorm"] + 3] = b_qnorm[0].T
    p[:, PC["b_knorm"]] = b_knorm
    p[:, PC["lb0"]:PC["lb0"] + 6] = colmaj(a_lb_logits[0])
    p[:, PC["lb1"]:PC["lb1"] + 6] = colmaj(a_lb_logits[1])
    return p


def build(ntiles=NT, nlayers=2):
    nc = bass.Bass("TRN2", target_bir_lowering=False)
    P = Prog(nc)

    xT = P.dram("xT", [D, S], F32, kind="ExternalInput")
    memT = P.dram("memT", [D, NMEM], F32, kind="ExternalInput")
    params = P.dram("params", [128, NPC], F32, kind="ExternalInput")
    cosT = P.dram("cosT", [128, S], F32, kind="ExternalInput")
    sinT = P.dram("sinT", [128, S], F32, kind="ExternalInput")
    outT = P.dram("outT", [D, S], F32, kind="ExternalOutput")
    cdram = {n: P.dram("c_" + n, sh, dt, kind="ExternalInput") for n, (sh, dt) in CONST_SHAPES.items()}
    wf = {}
    wb = {}
    for n, (kc, ncol) in GROUP_SHAPES.items():
        wf[n] = P.dram("wf_" + n, [128, kc * ncol], F32, kind="ExternalInput")
        if not n.startswith("mkv"):
            wb[n] = P.dram("wb_" + n, [128, kc * ncol], BF16)

    csb = {n: P.sbuf("k_" + n, sh, dt) for n, (sh, dt) in CONST_SHAPES.items()}
    par = P.sbuf("par", [128, NPC], F32)
    lbv = P.sbuf("lbv", [128, 12], F32)
    hT = P.sbuf("hT", [128, 8, T], F32)
    xn = P.sbuf("xn", [128, 8, T], BF16)
    mix = P.sbuf("mix", [128, 8, T], BF16)
    hTc = [Tile(hT.h, "hTc%d" % c) for c in range(8)]
    xnc = [Tile(xn.h, "xnc%d" % c) for c in range(8)]
    outc = [Tile(None, "outc%d" % c) for c in range(8)]
    P.tiles.extend(hTc + xnc + outc)
    CH = {id(hT): hTc, id(xn): xnc}

    def chunk_tile(t, c):
        lst = CH.get(id(t))
        return lst[c] if lst is not None else t
    hffn = P.sbuf("hffn", [128, NHC, T], BF16)
    wbuf = [P.sbuf("wbuf%d" % i, [128, 4096], BF16) for i in range(NWB)]
    KT = [P.sbuf("KT%d" % s, [128, 6, T], BF16) for s in range(NSLOT)]
    VV = [P.sbuf("VV%d" % s, [128, 4, 768], BF16) for s in range(NSLOT)]
    cos_t = P.sbuf("cos_t", [128, T], F32)
    sin_t = P.sbuf("sin_t", [128, T], F32)
    mkT = [P.sbuf("mkT%d" % l, [128, 2, NMEM], BF16) for l in range(2)]
    mva = [P.sbuf("mva%d" % l, [128, 2, 4, 128], BF16) for l in range(2)]
    Sst = [P.sbuf("S%d" % h, [128, 128], F32) for h in range(6)]
    Sbf = [[P.sbuf("Sbf%d_%d" % (h, i), [128, 128], BF16) for i in range(2)] for h in range(6)]
    NB2 = 2
    qin = [P.sbuf("qin%d" % i, [128, T], BF16) for i in range(NB2)]
    kin = [P.sbuf("kin%d" % i, [128, T], BF16) for i in range(NB2)]
    koT = [P.sbuf("koT%d" % i, [128, T], BF16) for i in range(NB2)]
    kotm = [P.sbuf("kotm%d" % i, [128, 4, 128], BF16) for i in range(NB2)]
    itm = [P.sbuf("itm%d" % i, [128, 4, 128], BF16) for i in range(NB2)]
    gT = [P.sbuf("gT%d" % i, [128, T], BF16) for i in range(NB2)]
    attT = [P.sbuf("attT%d" % i, [128, 4, 128], BF16) for i in range(NB2)]
    dech = [P.sbuf("dec%d" % i, [128, 8], F32) for i in range(NB2)]
    fA = [P.sbuf("fA%d" % i, [128, T], F32) for i in range(NB2)]
    fB = [P.sbuf("fB%d" % i, [128, T], F32) for i in range(NB2)]
    fC = [P.sbuf("fC%d" % i, [128, T], F32) for i in range(NB2)]
    sqb = [P.sbuf("sqb%d" % i, [128, T], BF16) for i in range(NB2)]
    knb = [P.sbuf("knb%d" % i, [128, T], BF16) for i in range(NB2)]
    qT = [[P.sbuf("qT%d_%d" % (i, g), [128, T], BF16) for g in range(3)] for i in range(NB2)]
    NPT = 7
    pT = [P.sbuf("pT%d" % i, [128, T], BF16) for i in range(5)]
    pTx = [Tile(hffn.h[:, NHC - 2 + i, :], "pTx%d" % i) for i in range(2)]
    P.tiles.extend(pTx)
    CH[id(hffn)] = [hffn] * (NHC - 2) + pTx
    pT.extend(pTx)
    rstd = fB[0]

    ps = [P.psum("ps%d" % i, [128, 512], F32) for i in range(8)]
    rr_state = {"ps": 0, "w": 0, "pt": 0}
    loaded = set()

    def next_ps():
        i = rr_state["ps"]
        rr_state["ps"] = (i + 1) % 5
        return ps[i]

    def next_ps3():
        i = rr_state["ps"] % 3
        rr_state["ps"] = (i + 1) % 3
        return ps[i]

    def next_pt():
        i = rr_state["pt"]
        rr_state["pt"] = (i + 1) % NPT
        return pT[i]

    def pcol(name, i=0):
        o = PC[name] + i
        return par[:, o:o + 1]

    def load_w(name):
        kc, ncol = GROUP_SHAPES[name]
        n = kc * ncol
        i = rr_state["w"]
        rr_state["w"] = (i + 1) % NWB
        buf = wbuf[i]
        if name not in loaded:
            loaded.add(name)
            P.dma("pool", buf[:, 0:n], wf[name][:], reads=[wf[name]], writes=[buf])
            if name in wb:
                P.dma("sp", wb[name][:], buf[:, 0:n], reads=[buf], writes=[wb[name]])
        else:
            P.dma("sp", buf[:, 0:n], wb[name][:], reads=[wb[name]], writes=[buf])
        view = buf[:, 0:n].rearrange("p (c n) -> p c n", c=kc)
        return buf, view

    def rstd_from_mean(ps_mean, n, out_t):
        P.add("act", lambda e: e.activation(out=out_t[:, 0:n], in_=ps_mean[:, 0:n], func=AF.Ln, bias=EPS),
              reads=[ps_mean], writes=[out_t])
        P.add("act", lambda e: e.activation(out=out_t[:, 0:n], in_=out_t[:, 0:n], func=AF.Exp, scale=-0.5),
              reads=[out_t], writes=[out_t])

    def norm_full(src, gain_name, gi, dst, n):
        for c in range(8):
            P.add("act", lambda e: e.activation(out=dst[:, c, 0:n], in_=src[:, c, 0:n], func=AF.Square),
                  reads=[chunk_tile(src, c)], writes=[chunk_tile(dst, c)])
        pm = next_ps()
        for c in range(8):
            P.add("pe", lambda e: e.matmul(pm[:, 0:n], lhsT=csb["ones_d"][:], rhs=dst[:, c, 0:n],
                                           start=(c == 0), stop=(c == 7)),
                  reads=[csb["ones_d"], chunk_tile(dst, c)], writes=[pm])
        rstd_from_mean(pm, n, rstd)
        for c in range(8):
            P.add("dve", lambda e: e.scalar_tensor_tensor(out=dst[:, c, 0:n], in0=src[:, c, 0:n],
                                                         scalar=pcol(gain_name, gi * 8 + c),
                                                         in1=rstd[:, 0:n], op0=ALU.mult, op1=ALU.mult),
                  reads=[chunk_tile(src, c), par, rstd], writes=[chunk_tile(dst, c)])

    def head_norm(psrc, n, ones_name, gain_ap, dst_ap, dst_tile, slot):
        sq = sqb[slot]
        P.add("act", lambda e: e.activation(out=sq[:, 0:n], in_=psrc[:, 0:n], func=AF.Square),
              reads=[psrc], writes=[sq])
        pm = next_ps()
        P.add("pe", lambda e: e.matmul(pm[:, 0:n], lhsT=csb[ones_name][:], rhs=sq[:, 0:n], start=True, stop=True),
              reads=[csb[ones_name], sq], writes=[pm])
        r = fB[slot]
        rstd_from_mean(pm, n, r)
        P.add("dve", lambda e: e.scalar_tensor_tensor(out=dst_ap, in0=psrc[:, 0:n], scalar=gain_ap,
                                                     in1=r[:, 0:n], op0=ALU.mult, op1=ALU.mult),
              reads=[psrc, par, r], writes=[dst_tile])

    def rope(src_bf, dst_ap, dst_tile, slot):
        pr = next_ps()
        P.add("pe", lambda e: e.matmul(pr[:], lhsT=csb["rrot"][:], rhs=src_bf[:], start=True, stop=True),
              reads=[csb["rrot"], src_bf], writes=[pr])
        t1 = fC[slot]
        t2 = fA[slot]
        P.add("dve", lambda e: e.tensor_tensor(out=t1[:], in0=src_bf[:], in1=cos_t[:], op=ALU.mult),
              reads=[src_bf, cos_t], writes=[t1])
        P.add("dve", lambda e: e.tensor_tensor(out=t2[:], in0=pr[:], in1=sin_t[:], op=ALU.mult),
              reads=[pr, sin_t], writes=[t2])
        P.add("dve", lambda e: e.tensor_tensor(out=dst_ap, in0=t1[:], in1=t2[:], op=ALU.add),
              reads=[t1, t2], writes=[dst_tile])

    def head_norm_g(psrc, n, ones_name, gain_ap, dst_ap, dst_tile, slot):
        sq = sqb[slot]
        P.add("act", lambda e: e.activation(out=sq[:, 0:n], in_=psrc[:, 0:n], func=AF.Square),
              reads=[psrc], writes=[sq])
        yield
        pm = next_ps()
        P.add("pe", lambda e: e.matmul(pm[:, 0:n], lhsT=csb[ones_name][:], rhs=sq[:, 0:n], start=True, stop=True),
              reads=[csb[ones_name], sq], writes=[pm])
        r = fB[slot]
        rstd_from_mean(pm, n, r)
        P.add("dve", lambda e: e.scalar_tensor_tensor(out=dst_ap, in0=psrc[:, 0:n], scalar=gain_ap,
                                                     in1=r[:, 0:n], op0=ALU.mult, op1=ALU.mult),
              reads=[psrc, par, r], writes=[dst_tile])
        t1 = fC[slot]
        yield

    def rope_g(src_bf, dst_ap, dst_tile, slot):
        t1 = fC[slot]
        t2 = fA[slot]
        P.add("dve", lambda e: e.tensor_tensor(out=t1[:], in0=src_bf[:], in1=cos_t[:], op=ALU.mult),
              reads=[src_bf, cos_t], writes=[t1])
        yield
        pr = next_ps()
        P.add("pe", lambda e: e.matmul(pr[:], lhsT=csb["rrot"][:], rhs=src_bf[:], start=True, stop=True),
              reads=[csb["rrot"], src_bf], writes=[pr])
        P.add("dve", lambda e: e.tensor_tensor(out=t2[:], in0=pr[:], in1=sin_t[:], op=ALU.mult),
              reads=[pr, sin_t], writes=[t2])
        P.add("dve", lambda e: e.tensor_tensor(out=dst_ap, in0=t1[:], in1=t2[:], op=ALU.add),
              reads=[t1, t2], writes=[dst_tile])
        yield

    def qk_prep(psrc, gain_ap, dst_ap, dst_tile, slot):
        xg = knb[slot]
        sq = sqb[slot]
        P.add("act", lambda e: e.activation(out=xg[:], in_=psrc[:], func=AF.Copy, scale=gain_ap),
              reads=[psrc, par], writes=[xg])
        P.add("act", lambda e: e.activation(out=sq[:], in_=psrc[:], func=AF.Square), reads=[psrc], writes=[sq])
        yield
        pm = next_ps()
        P.add("pe", lambda e: e.matmul(pm[:], lhsT=csb["ones_h"][:], rhs=sq[:], start=True, stop=True),
              reads=[csb["ones_h"], sq], writes=[pm])
        pr = next_ps()
        P.add("pe", lambda e: e.matmul(pr[:], lhsT=csb["rrot"][:], rhs=xg[:], start=True, stop=True),
              reads=[csb["rrot"], xg], writes=[pr])
        r = fB[slot]
        rstd_from_mean(pm, T, r)
        t1 = fC[slot]
        t2 = fA[slot]
        P.add("dve", lambda e: e.tensor_tensor(out=t1[:], in0=xg[:], in1=cos_t[:], op=ALU.mult),
              reads=[xg, cos_t], writes=[t1])
        yield
        P.add("dve", lambda e: e.tensor_tensor(out=t2[:], in0=pr[:], in1=sin_t[:], op=ALU.mult),
              reads=[pr, sin_t], writes=[t2])
        P.add("dve", lambda e: e.tensor_tensor(out=t1[:], in0=t1[:], in1=t2[:], op=ALU.add),
              reads=[t1, t2], writes=[t1])
        P.add("dve", lambda e: e.tensor_tensor(out=dst_ap, in0=t1[:], in1=r[:], op=ALU.mult),
              reads=[t1, r], writes=[dst_tile])
        yield

    def lin_fm(wtile, wview, col0, rhs_tile, kc, out_ps, n, c0=0, c1=None):
        for c in range(c0, kc if c1 is None else c1):
            P.add("pe", lambda e: e.matmul(out_ps[:, 0:n], lhsT=wview[:, c, col0:col0 + 128], rhs=rhs_tile[:, c, 0:n],
                                           start=(c == 0), stop=(c == kc - 1)),
                  reads=[wtile, chunk_tile(rhs_tile, c)], writes=[out_ps])

    def interleave(*gens):
        gens = [g for g in gens if g is not None]
        while gens:
            for g in list(gens):
                try:
                    next(g)
                except StopIteration:
                    gens.remove(g)

    for n in CONST_SHAPES:
        P.dma("sp", csb[n][:], cdram[n][:], reads=[cdram[n]], writes=[csb[n]])
    P.dma("sp", par[:], params[:], reads=[params], writes=[par])
    P.dma("sp", hT[:, :, 0:NMEM], memT.h.rearrange("(c p) t -> p c t", p=128), reads=[memT], writes=hTc,
          sem_tile=hTc[0])

    P.add("dve", lambda e: e.tensor_tensor(out=lbv[:, 0:6], in0=par[:, PC["lb0"]:PC["lb0"] + 6],
                                          in1=par[:, PC["lb1"]:PC["lb1"] + 6], op=ALU.subtract),
          reads=[par], writes=[lbv])
    P.add("act", lambda e: e.activation(out=lbv[:, 0:6], in_=lbv[:, 0:6], func=AF.Sigmoid),
          reads=[lbv], writes=[lbv])
    P.add("dve", lambda e: e.tensor_scalar(out=lbv[:, 6:12], in0=lbv[:, 0:6], scalar1=-1.0, scalar2=1.0,
                                          op0=ALU.mult, op1=ALU.add),
          reads=[lbv], writes=[lbv])
    for h in range(6):
        P.add("dve", lambda e: e.memset(Sst[h][:], 0.0), writes=[Sst[h]])
        P.add("dve", lambda e: e.memset(Sbf[h][0][:], 0.0), writes=[Sbf[h][0]])

    for l in range(nlayers):
        norm_full(hT, "mem_norm", l, xn, NMEM)
        wt, wv = load_w("mkv%d" % l)
        P.add("dve", lambda e: e.memset(mva[l][:], 0.0), writes=[mva[l]])
        for j in range(2):
            pk = next_ps()
            lin_fm(wt, wv, j * 128, xn, 8, pk, NMEM)
            head_norm(pk, NMEM, "ones_64", pcol("mem_knorm", l), mkT[l][:, j, :], mkT[l], 0)
        for s in range(2):
            pv = next_ps()
            for c in range(8):
                P.add("pe", lambda e: e.matmul(pv[:, 0:256], lhsT=xn[:, c, s * 128:(s + 1) * 128],
                                               rhs=wv[:, c, 256:512], start=(c == 0), stop=(c == 7)),
                      reads=[wt, xnc[c]], writes=[pv])
            for hm in range(4):
                o = (hm % 2) * 64
                P.add("act", lambda e: e.activation(out=mva[l][:, s, hm, o:o + 64], in_=pv[:, hm * 64:(hm + 1) * 64],
                                                    func=AF.Copy),
                      reads=[pv], writes=[mva[l]])

    def memory_attention(l, wname):
        wt, wv = load_w(wname)
        banks = [ps[7], ps[4]] if l == 0 else [ps[7]]
        st = {"i": 0}

        def mtrans():
            b = banks[st["i"] % len(banks)]
            st["i"] += 1
            return b

        xg = qT[1][0]
        mqn = knb[1]
        sq = sqb[1]
        r = fB[1]
        rec = fC[1]
        pnum = ps[5]
        pden = ps[6]
        for j in range(2):
            pq = mtrans()
            lin_fm(wt, wv, j * 128, xn, 8, pq, T)
            P.add("act", lambda e: e.activation(out=xg[:], in_=pq[:], func=AF.Copy, scale=pcol("mem_qnorm", l)),
                  reads=[pq, par], writes=[xg])
            P.add("act", lambda e: e.activation(out=sq[:], in_=pq[:], func=AF.Square), reads=[pq], writes=[sq])
            yield
            pm = mtrans()
            P.add("pe", lambda e: e.matmul(pm[:], lhsT=csb["ones_64"][:], rhs=sq[:], start=True, stop=True),
                  reads=[csb["ones_64"], sq], writes=[pm])
            rstd_from_mean(pm, T, r)
            P.add("dve", lambda e: e.tensor_tensor(out=mqn[:], in0=xg[:], in1=r[:], op=ALU.mult),
                  reads=[xg, r], writes=[mqn])
            yield
            pend = []
            items = [(hm, s) for hm in (2 * j, 2 * j + 1) for s in range(2)]
            cnt = {"n": 0}

            def flush(it, last):
                hm, s, pt = it
                first = cnt["n"] == 0
                cnt["n"] += 1
                P.add("pe", lambda e: e.matmul(pnum[:], lhsT=mva[l][:, s, hm, :], rhs=pt[:],
                                               start=first, stop=last, skip_group_check=True),
                      reads=[mva[l], pt], writes=[pnum])
                P.add("pe", lambda e: e.matmul(pden[:], lhsT=csb["blk_ones"][:, hm % 2, :], rhs=pt[:],
                                               start=first, stop=last, skip_group_check=True),
                      reads=[csb["blk_ones"], pt], writes=[pden])

            for (hm, s) in items:
                o = (hm % 2) * 64
                pscore = mtrans()
                P.add("pe", lambda e: e.matmul(pscore[:], lhsT=mkT[l][o:o + 64, j, s * 128:(s + 1) * 128],
                                               rhs=mqn[o:o + 64, :], start=True, stop=True),
                      reads=[mkT[l], mqn], writes=[pscore])
                pt = next_pt()
                P.add("act", lambda e: e.activation(out=pt[:], in_=pscore[:], func=AF.Exp, scale=0.125),
                      reads=[pscore], writes=[pt])
                pend.append((hm, s, pt))
                if len(pend) > 2:
                    flush(pend.pop(0), False)
                yield
            while pend:
                flush(pend.pop(0), len(pend) == 0)
            P.add("act", lambda e: e.activation(out=rec[:], in_=pden[:], func=AF.Ln), reads=[pden], writes=[rec])
            P.add("act", lambda e: e.activation(out=rec[:], in_=rec[:], func=AF.Exp, scale=-1.0),
                  reads=[rec], writes=[rec])
            P.add("dve", lambda e: e.tensor_tensor(out=mix[:, 6 + j, :], in0=pnum[:], in1=rec[:], op=ALU.mult),
                  reads=[pnum, rec], writes=[mix])
            yield

    def out_proj_and_ffn(l, store_p0=None):
        for half in range(2):
            wt, wv = load_w("wo%d_%d" % (l, half))
            for jj in range(4):
                j = half * 4 + jj
                po = next_ps()
                lin_fm(wt, wv, jj * 128, mix, 8, po, T)
                P.add("dve", lambda e: e.tensor_tensor(out=hT[:, j, :], in0=po[:], in1=hT[:, j, :], op=ALU.add),
                      reads=[po, hTc[j]], writes=[hTc[j]])
        norm_full(hT, "norm_ffn", l, xn, T)
        for i in range(NHC // 2):
            wt, wv = load_w("gu%d_%d" % (l, i))
            for k in range(2):
                hc = 2 * i + k
                pg = next_ps()
                pu = next_ps()
                lin_fm(wt, wv, k * 256, xn, 8, pg, T)
                lin_fm(wt, wv, k * 256 + 128, xn, 8, pu, T)
                sg = fA[hc % 2]
                P.add("act", lambda e: e.activation(out=sg[:], in_=pg[:], func=AF.Silu), reads=[pg], writes=[sg])
                P.add("dve", lambda e: e.tensor_tensor(out=hffn[:, hc, :], in0=pu[:], in1=sg[:], op=ALU.mult),
                      reads=[pu, sg], writes=[chunk_tile(hffn, hc)])
        for j in range(8):
            wt, wv = load_w("wd%d_%d" % (l, j))
            po = next_ps()
            lin_fm(wt, wv, 0, hffn, NHC, po, T)
            P.add("dve", lambda e: e.tensor_tensor(out=hT[:, j, :], in0=po[:], in1=hT[:, j, :], op=ALU.add),
                  reads=[po, hTc[j]], writes=[hTc[j]])
            if store_p0 is not None:
                P.dma("pool", outT[j * 128:(j + 1) * 128, store_p0:store_p0 + T], hT[:, j, :], reads=[hTc[j]],
                      writes=[outc[j]], sem_tile=outc[j])

    pdS = {}

    def hgrn_A(h):
        sl = h % 2
        wt, wv = load_w("a%d" % h)
        a, b_, c_ = fA[sl], fB[sl], fC[sl]
        pi = ps[3]

        def pi_mm(s):
            for c in range(8):
                P.add("pe", lambda e: e.matmul(pi[:, s * 128:(s + 1) * 128], lhsT=xn[:, c, s * 128:(s + 1) * 128],
                                               rhs=wv[:, c, 256:384], start=(c == 0), stop=(c == 7)),
                      reads=[wt, xnc[c]], writes=[pi])

        pf = next_ps3()
        lin_fm(wt, wv, 128, xn, 8, pf, T, 0, 4)
        yield
        lin_fm(wt, wv, 128, xn, 8, pf, T, 4, 8)
        P.add("act", lambda e: e.activation(out=a[:], in_=pf[:], func=AF.Sigmoid), reads=[pf], writes=[a])
        yield
        pq = next_ps3()
        lin_fm(wt, wv, 0, xn, 8, pq, T, 0, 4)
        yield
        lin_fm(wt, wv, 0, xn, 8, pq, T, 4, 8)
        P.add("dve", lambda e: e.tensor_scalar(out=a[:], in0=a[:], scalar1=lbv[:, 6 + h:7 + h],
                                              scalar2=lbv[:, h:h + 1], op0=ALU.mult, op1=ALU.add),
              reads=[a, lbv], writes=[a])
        P.add("act", lambda e: e.activation(out=b_[:], in_=a[:], func=AF.Ln), reads=[a], writes=[b_])
        yield
        pi_mm(0)
        P.add("dve", lambda e: e.tensor_tensor_scan(out=c_[:], data0=csb["rmask"][:], data1=b_[:],
                                                   initial=0.0, op0=ALU.mult, op1=ALU.add),
              reads=[csb["rmask"], b_], writes=[c_])
        P.add("dve", lambda e: e.tensor_scalar(out=a[:], in0=a[:], scalar1=-1.0, scalar2=1.0,
                                              op0=ALU.mult, op1=ALU.add),
              reads=[a], writes=[a])
        P.add("act", lambda e: e.activation(out=b_[:], in_=c_[:], func=AF.Exp), reads=[c_], writes=[b_])
        P.add("act", lambda e: e.activation(out=c_[:], in_=c_[:], func=AF.Exp, scale=-1.0),
              reads=[c_], writes=[c_])
        yield
        pi_mm(1)
        P.add("dve", lambda e: e.tensor_copy(out=dech[sl][:], in_=b_[:].rearrange("p (c k) -> p c k", k=64)[:, :, 63]),
              reads=[b_], writes=[dech[sl]])
        P.add("dve", lambda e: e.tensor_tensor(out=kin[sl][:], in0=a[:], in1=c_[:], op=ALU.mult),
              reads=[a, c_], writes=[kin[sl]])
        P.add("act", lambda e: e.activation(out=a[:], in_=pq[:], func=AF.Silu), reads=[pq], writes=[a])
        yield
        pi_mm(2)
        for cc in range(8):
            P.add("dve", lambda e: e.tensor_scalar(out=koT[sl][:, cc * 64:(cc + 1) * 64],
                                                  in0=kin[sl][:, cc * 64:(cc + 1) * 64],
                                                  scalar1=dech[sl][:, cc:cc + 1], scalar2=None, op0=ALU.mult),
                  reads=[kin[sl], dech[sl]], writes=[koT[sl]])
        P.add("dve", lambda e: e.tensor_tensor(out=qin[sl][:], in0=a[:], in1=b_[:], op=ALU.mult),
              reads=[a, b_], writes=[qin[sl]])
        yield
        pi_mm(3)
        P.add("act", lambda e: e.activation(out=itm[sl][:].rearrange("p s n -> p (s n)"), in_=pi[:], func=AF.Copy),
              reads=[pi], writes=[itm[sl]])
        yield
        trt = next_ps3()
        trv = trt.h.bitcast(BF16)
        for s in range(4):
            P.add("pe", lambda e: e.transpose(trv[:, s * 128:(s + 1) * 128], koT[sl][:, s * 128:(s + 1) * 128],
                                              csb["ident"][:]),
                  reads=[koT[sl], csb["ident"]], writes=[trt])
        P.add("act", lambda e: e.activation(out=kotm[sl][:].rearrange("p s n -> p (s n)"), in_=trv[:, 0:512],
                                            func=AF.Copy),
              reads=[trt], writes=[kotm[sl]])
        yield
        pa = next_ps3()
        for s in range(4):
            P.add("pe", lambda e: e.matmul(pa[:, s * 128:(s + 1) * 128], lhsT=kin[sl][:, s * 128:(s + 1) * 128],
                                           rhs=qin[sl][:, s * 128:(s + 1) * 128], start=True, stop=True),
                  reads=[kin[sl], qin[sl]], writes=[pa])
        P.add("dve", lambda e: e.tensor_tensor(out=attT[sl][:].rearrange("p s n -> p (s n)"), in0=pa[:],
                                              in1=csb["cmask"][:].rearrange("p s n -> p (s n)"), op=ALU.mult),
              reads=[pa, csb["cmask"]], writes=[attT[sl]])
        yield
        pg = next_ps3()
        lin_fm(wt, wv, 384, xn, 8, pg, T)
        P.add("act", lambda e: e.activation(out=gT[sl][:], in_=pg[:], func=AF.Silu), reads=[pg], writes=[gT[sl]])
        yield

    def hgrn_B(h):
        sl = h % 2
        po = ps[6 + (h % 2)]
        pd = [ps[4], ps[5]]
        for cc in range(8):
            s = cc // 2
            o = (cc % 2) * 64
            P.add("pe", lambda e: e.matmul(pd[cc % 2][:, (cc // 2) * 128:(cc // 2 + 1) * 128],
                                           lhsT=kotm[sl][o:o + 64, s, :], rhs=itm[sl][o:o + 64, s, :],
                                           start=True, stop=True),
                  reads=[kotm[sl], itm[sl]], writes=[pd[cc % 2]])
        yield
        for cc in range(8):
            s = cc // 2
            o = (cc % 2) * 64
            cs_ = slice(cc * 64, (cc + 1) * 64)
            cur = Sbf[h][cc % 2]
            nxt = Sbf[h][(cc + 1) % 2]
            P.add("pe", lambda e: e.matmul(po[:, cs_], lhsT=cur[:], rhs=qin[sl][:, cs_],
                                           start=True, stop=False, skip_group_check=True),
                  reads=[cur, qin[sl]], writes=[po])
            P.add("pe", lambda e: e.matmul(po[:, cs_], lhsT=itm[sl][o:o + 64, s, :], rhs=attT[sl][o:o + 64, s, o:o + 64],
                                           start=False, stop=True, skip_group_check=True),
                  reads=[itm[sl], attT[sl]], writes=[po])
            pdc = pd[cc % 2][:, (cc // 2) * 128:(cc // 2 + 1) * 128]
            P.add("dve", lambda e: e.scalar_tensor_tensor(out=nxt[:], in0=Sst[h][:], scalar=dech[sl][:, cc:cc + 1],
                                                         in1=pdc, op0=ALU.mult, op1=ALU.add),
                  reads=[Sst[h], dech[sl], pd[cc % 2]], writes=[nxt])
            P.add("dve", lambda e: e.scalar_tensor_tensor(out=Sst[h][:], in0=Sst[h][:], scalar=dech[sl][:, cc:cc + 1],
                                                         in1=pdc, op0=ALU.mult, op1=ALU.add),
                  reads=[Sst[h], dech[sl], pd[cc % 2]], writes=[Sst[h]])
            yield
        sq = sqb[sl]
        P.add("act", lambda e: e.activation(out=sq[:], in_=po[:], func=AF.Square), reads=[po], writes=[sq])
        pm = next_ps3()
        P.add("pe", lambda e: e.matmul(pm[:], lhsT=csb["ones_h"][:], rhs=sq[:], start=True, stop=True),
              reads=[csb["ones_h"], sq], writes=[pm])
        r = fB[sl]
        rstd_from_mean(pm, T, r)
        a = fA[sl]
        P.add("dve", lambda e: e.scalar_tensor_tensor(out=a[:], in0=po[:], scalar=pcol("a_onorm", h), in1=r[:],
                                                     op0=ALU.mult, op1=ALU.mult),
              reads=[po, par, r], writes=[a])
        P.add("dve", lambda e: e.tensor_tensor(out=mix[:, h, :], in0=a[:], in1=gT[sl][:], op=ALU.mult),
              reads=[a, gT[sl]], writes=[mix])
        yield

    def attn_A(t, h):
        sl = h % 2
        wt, wv = load_w("b%d" % h)
        for g in range(3):
            pq = next_ps()
            lin_fm(wt, wv, g * 128, xn, 8, pq, T)
            for _ in qk_prep(pq, pcol("b_qnorm", g), qT[sl][g][:], qT[sl][g], sl):
                yield

    def attn_B(t, h, LOOK=5):
        sl = h % 2
        p0 = t * T
        pnum = ps[5] if h % 2 == 0 else ps[7]
        pden = ps[6]
        blocks = []
        for g, dil in enumerate(DILS):
            for kb in range(max(0, 4 * t - dil), 4 * t + 4):
                kj = 128 * kb
                qlo = max(0, kj - p0)
                qhi = min(T, kj - p0 + 128 * dil + 128)
                if qhi <= qlo:
                    continue
                blocks.append((g, dil, kb, qlo, qhi, p0 + qlo - kj))
        pend = []
        cnt = {"n": 0}

        def flush(it, last):
            g, dil, kb, qlo, qhi, x0, pt = it
            first = cnt["n"] == 0
            cnt["n"] += 1
            ks = (kb // 4) % NSLOT
            sub = kb % 4
            P.add("pe", lambda e: e.matmul(pnum[:, qlo:qhi], lhsT=VV[ks][:, sub, h * 128:(h + 1) * 128],
                                           rhs=pt[:, qlo:qhi], start=first, stop=last, skip_group_check=True),
                  reads=[VV[ks], pt], writes=[pnum])
            P.add("pe", lambda e: e.matmul(pden[:, qlo:qhi], lhsT=csb["ones_f"][:], rhs=pt[:, qlo:qhi],
                                           start=first, stop=last, skip_group_check=True),
                  reads=[csb["ones_f"], pt], writes=[pden])

        for bi, (g, dil, kb, qlo, qhi, x0) in enumerate(blocks):
            ks = (kb // 4) % NSLOT
            sub = kb % 4
            pscore = next_ps()
            P.add("pe", lambda e: e.matmul(pscore[:, qlo:qhi], lhsT=KT[ks][:, h, sub * 128:(sub + 1) * 128],
                                           rhs=qT[sl][g][:, qlo:qhi], start=True, stop=True),
                  reads=[KT[ks], qT[sl][g]], writes=[pscore])
            pt = next_pt()
            P.add("act", lambda e: e.activation(out=pt[:, qlo:qhi], in_=pscore[:, qlo:qhi], func=AF.Exp,
                                                scale=float(128 ** -0.5)),
                  reads=[pscore], writes=[pt])
            mk = csb["dmask%d" % dil]
            P.add("dve", lambda e: e.tensor_tensor(out=pt[:, qlo:qhi], in0=pt[:, qlo:qhi],
                                                  in1=mk[:, x0:x0 + (qhi - qlo)], op=ALU.mult),
                  reads=[pt, mk], writes=[pt])
            pend.append((g, dil, kb, qlo, qhi, x0, pt))
            if len(pend) > LOOK:
                flush(pend.pop(0), False)
            yield
        while pend:
            flush(pend.pop(0), len(pend) == 0)
        rec = fC[sl]
        P.add("act", lambda e: e.activation(out=rec[:], in_=pden[:], func=AF.Ln), reads=[pden], writes=[rec])
        P.add("act", lambda e: e.activation(out=rec[:], in_=rec[:], func=AF.Exp, scale=-1.0),
              reads=[rec], writes=[rec])
        P.add("dve", lambda e: e.tensor_tensor(out=mix[:, h, :], in0=pnum[:], in1=rec[:], op=ALU.mult),
              reads=[pnum, rec], writes=[mix])
        yield

    for t in range(ntiles):
        p0 = t * T
        slot = t % NSLOT
        for c in range(8):
            P.dma("sp", hT[:, c, :], xT[c * 128:(c + 1) * 128, p0:p0 + T], reads=[xT], writes=[hTc[c]])
        P.dma("sp", cos_t[:], cosT[:, p0:p0 + T], reads=[cosT], writes=[cos_t])
        P.dma("sp", sin_t[:], sinT[:, p0:p0 + T], reads=[sinT], writes=[sin_t])

        norm_full(hT, "norm_mix", 0, xn, T)
        interleave(memory_attention(0, "amq"), hgrn_A(0))
        for h in range(6):
            interleave(hgrn_B(h), hgrn_A(h + 1) if h < 5 else None)
        out_proj_and_ffn(0, p0 if nlayers == 1 else None)

        if nlayers > 1:
            norm_full(hT, "kv_norm", 0, xn, T)
            wt0, wv0 = load_w("kv0")
            wt1, wv1 = load_w("kv1")
            wt2, wv2 = load_w("kv2")

            def kgen():
                for h in range(6):
                    sl = h % 2
                    wt, wv = (wt0, wv0) if h < 4 else (wt1, wv1)
                    pk = next_ps()
                    lin_fm(wt, wv, (h % 4) * 128, xn, 8, pk, T)
                    for _ in qk_prep(pk, pcol("b_knorm"), KT[slot][:, h, :], KT[slot], sl):
                        yield

            def vgen():
                for s in range(4):
                    pv = next_ps()
                    for c in range(8):
                        P.add("pe", lambda e: e.matmul(pv[:, 0:256], lhsT=xn[:, c, s * 128:(s + 1) * 128],
                                                       rhs=wv1[:, c, 256:512], start=(c == 0), stop=(c == 7)),
                              reads=[wt1, xnc[c]], writes=[pv])
                    P.add("act", lambda e: e.activation(out=VV[slot][:, s, 0:256], in_=pv[:, 0:256], func=AF.Copy),
                          reads=[pv], writes=[VV[slot]])
                    yield
                    pv2 = next_ps()
                    for c in range(8):
                        P.add("pe", lambda e: e.matmul(pv2[:], lhsT=xn[:, c, s * 128:(s + 1) * 128],
                                                       rhs=wv2[:, c, :], start=(c == 0), stop=(c == 7)),
                              reads=[wt2, xnc[c]], writes=[pv2])
                    P.add("act", lambda e: e.activation(out=VV[slot][:, s, 256:768], in_=pv2[:], func=AF.Copy),
                          reads=[pv2], writes=[VV[slot]])
                    yield

            interleave(kgen(), vgen())

            norm_full(hT, "norm_mix", 1, xn, T)
            interleave(memory_attention(1, "bmq"), attn_A(t, 0))
            nblk = sum(1 for dil in DILS for kb in range(max(0, 4 * t - dil), 4 * t + 4))
            kk = max(1, min(4, nblk // 9))
            for h in range(6):
                gb = attn_B(t, h)
                ga = attn_A(t, h + 1) if h < 5 else None
                i = 0
                for _ in gb:
                    i += 1
                    if ga is not None and i % kk == 0:
                        try:
                            next(ga)
                        except StopIteration:
                            ga = None
                if ga is not None:
                    for _ in ga:
                        pass
            out_proj_and_ffn(1, p0)


    P.emit(final_wait_tiles=outc)
    P.close()
    return nc


_CACHE = {}


def kernel(x, mem, norm_mix, norm_ffn, a_w_in, a_lb_logits, a_onorm, b_w_in, b_qnorm, kv_norm, w_kv,
           b_knorm, mem_norm, w_mem_kv, mem_qnorm, mem_knorm, w_out, w_gate_up, w_down, _ntiles=NT, _nlayers=2):
    f = lambda a: np.asarray(a, dtype=np.float32)
    x, mem = f(x), f(mem)
    groups = weight_groups(f(a_w_in), f(b_w_in), f(w_kv), f(w_mem_kv), f(w_out), f(w_gate_up), f(w_down))
    consts = host_consts()
    params = host_params(f(norm_mix), f(norm_ffn), f(a_lb_logits), f(a_onorm), f(b_qnorm), f(kv_norm),
                         f(b_knorm), f(mem_norm), f(mem_qnorm), f(mem_knorm))
    key = (_ntiles, _nlayers)
    if key not in _CACHE:
        _CACHE[key] = build(_ntiles, _nlayers)
    nc = _CACHE[key]
    B = x.shape[0]
    shared = {"params": params, "cosT": consts["cosT"], "sinT": consts["sinT"]}
    for n in CONST_SHAPES:
        shared["c_" + n] = consts[n]
    for n, (arr, kc, ncol) in groups.items():
        shared["wf_" + n] = arr
    in_maps = []
    for b in range(B):
        m = dict(shared)
        m["xT"] = np.ascontiguousarray(x[b].T)
        m["memT"] = np.ascontiguousarray(mem[b].T)
        in_maps.append(m)
    res = run_bass_kernel_spmd(nc, in_maps, core_ids=list(range(B)))
    out = np.stack([np.ascontiguousarray(r["outT"].T) for r in res.results], axis=0)
    return out.astype(np.float32)
```

```python
from contextlib import ExitStack
import numpy as np
import ml_dtypes
import concourse.bass as bass
import concourse.mybir as mybir
from concourse.bass_utils import run_bass_kernel_spmd

F32 = mybir.dt.float32
BF16 = mybir.dt.bfloat16
AF = mybir.ActivationFunctionType
ALU = mybir.AluOpType

COMPUTE = ("pe", "act", "dve", "pool")

D = 1024
S = 4096
T = 512
NT = S // T
NMEM = 256
FFN = 2816
NHC = FFN // 128
EPS = 1e-6
DILS = (1, 4, 16)
NSLOT = 5
NWB = 3


class Tile:
    def __init__(self, h, name):
        self.h = h
        self.name = name
        self.w_by = {}
        self.r_by = {}
        self.sems = {}
        self.sem_count = {}

    def __getitem__(self, idx):
        return self.h[idx]


class Op:
    __slots__ = ("eng", "fn", "deps", "needs_inc", "tick", "is_dma", "sem_tile", "dma_count", "key", "qc")

    def __init__(self, eng, fn):
        self.eng = eng
        self.fn = fn
        self.deps = []
        self.needs_inc = False
        self.tick = None
        self.is_dma = False
        self.sem_tile = None
        self.dma_count = None
        self.key = eng
        self.qc = None


class _Rec:
    def __init__(self):
        self.calls = []

    def __getattr__(self, name):
        def f(*a, **k):
            self.calls.append((name, a, k))
            return self
        return f


class Prog:
    def __init__(self, nc):
        self.nc = nc
        self.es = ExitStack()
        self.ops = {e: [] for e in ("pe", "act", "dve", "pool", "sp")}
        self.tiles = []

    def sbuf(self, name, shape, dtype):
        h = self.es.enter_context(self.nc.sbuf_tensor(name, list(shape), dtype))
        t = Tile(h, name)
        self.tiles.append(t)
        return t

    def psum(self, name, shape, dtype=F32):
        h = self.es.enter_context(self.nc.psum_tensor(name, list(shape), dtype))
        t = Tile(h, name)
        self.tiles.append(t)
        return t

    def dram(self, name, shape, dtype, kind="Internal"):
        h = self.nc.dram_tensor(name, list(shape), dtype, kind=kind)
        t = Tile(h, name)
        self.tiles.append(t)
        return t

    def _track(self, op, reads, writes):
        deps = {}
        for t in reads:
            for w in t.w_by.values():
                deps[id(w)] = w
        for t in writes:
            for w in t.w_by.values():
                deps[id(w)] = w
            for r in t.r_by.values():
                deps[id(r)] = r
        for d in deps.values():
            if d is op:
                continue
            if d.eng == "pe" and op.eng == "pe":
                continue
            op.deps.append(d)
            d.needs_inc = True
        for t in reads:
            t.r_by[op.key] = op
        for t in writes:
            t.w_by[op.key] = op

    def add(self, eng, fn, reads=(), writes=()):
        rec = _Rec()
        fn(rec)
        assert len(rec.calls) == 1
        name, a, k = rec.calls[0]
        op = Op(eng, lambda e: getattr(e, name)(*a, **k))
        self._track(op, reads, writes)
        self.ops[eng].append(op)
        return op

    def dma(self, queue, out_ap, in_ap, reads=(), writes=(), sem_tile=None, **kw):
        if sem_tile is None:
            sem_tile = writes[0]
        op = Op(queue, lambda e: e.dma_start(out=out_ap, in_=in_ap, **kw))
        op.is_dma = True
        op.sem_tile = sem_tile
        qc = "sw" if queue == "pool" else "hw"
        op.qc = qc
        op.key = ("dma", id(sem_tile), qc)
        sem_tile.sem_count[qc] = sem_tile.sem_count.get(qc, 0) + 1
        op.dma_count = sem_tile.sem_count[qc]
        op.needs_inc = True
        self._track(op, reads, writes)
        self.ops[queue].append(op)
        return op

    def emit(self, final_wait_tiles=()):
        nc = self.nc
        es = self.es
        esem = {e: es.enter_context(nc.semaphore("sem_" + e)) for e in COMPUTE}
        for t in self.tiles:
            for qc in t.sem_count:
                t.sems[qc] = es.enter_context(nc.semaphore("dsem_%s_%s" % (qc, t.name)))
        for e in COMPUTE:
            c = 0
            for op in self.ops[e]:
                if op.is_dma:
                    continue
                if op.needs_inc:
                    c += 1
                    op.tick = c
        block = es.enter_context(nc.Block())

        def target(d):
            if d.is_dma:
                return d.sem_tile.sems[d.qc], 16 * d.dma_count
            return esem[d.eng], d.tick

        def run(ename, eng):
            waited = {}
            for op in self.ops[ename]:
                need = {}
                for d in op.deps:
                    s, v = target(d)
                    k = id(s)
                    if waited.get(k, 0) >= v:
                        continue
                    if k not in need or need[k][1] < v:
                        need[k] = (s, v)
                for k, (s, v) in need.items():
                    eng.wait_ge(s, v)
                    waited[k] = v
                ins = op.fn(eng)
                if op.is_dma:
                    ins.then_inc(op.sem_tile.sems[op.qc], 16)
                elif op.needs_inc:
                    ins.then_inc(esem[op.eng], 1)
            if ename == "sp":
                for t in final_wait_tiles:
                    for qc, sm in t.sems.items():
                        eng.wait_ge(sm, 16 * t.sem_count[qc])

        @block.sync
        def _(e):
            run("sp", e)

        @block.tensor
        def _(e):
            run("pe", e)

        @block.scalar
        def _(e):
            run("act", e)

        @block.vector
        def _(e):
            run("dve", e)

        @block.gpsimd
        def _(e):
            run("pool", e)

    def close(self):
        self.es.close()


def _grp(wsub):
    K, n = wsub.shape
    kc = K // 128
    return np.ascontiguousarray(wsub.reshape(kc, 128, n).transpose(1, 0, 2).reshape(128, kc * n))


def weight_groups(a_w_in, b_w_in, w_kv, w_mem_kv, w_out, w_gate_up, w_down):
    g = {}
    A = a_w_in[0]
    for h in range(6):
        cols = np.concatenate([np.arange(k * 768 + h * 128, k * 768 + (h + 1) * 128) for k in range(4)])
        g["a%d" % h] = (_grp(A[:, cols]), 8, 512)
    g["amq"] = (_grp(A[:, 3072:3328]), 8, 256)
    Bw = b_w_in[0]
    for h in range(6):
        cols = np.concatenate([np.arange(k * 768 + h * 128, k * 768 + (h + 1) * 128) for k in range(3)])
        g["b%d" % h] = (_grp(Bw[:, cols]), 8, 384)
    g["bmq"] = (_grp(Bw[:, 2304:2560]), 8, 256)
    g["kv0"] = (_grp(w_kv[:, 0:512]), 8, 512)
    g["kv1"] = (_grp(w_kv[:, 512:1024]), 8, 512)
    g["kv2"] = (_grp(w_kv[:, 1024:1536]), 8, 512)
    for l in range(2):
        g["mkv%d" % l] = (_grp(w_mem_kv[l]), 8, 512)
        g["wo%d_0" % l] = (_grp(w_out[l][:, 0:512]), 8, 512)
        g["wo%d_1" % l] = (_grp(w_out[l][:, 512:1024]), 8, 512)
        for i in range(NHC // 2):
            cols = np.concatenate([np.arange(hc * 128, (hc + 1) * 128) + off
                                   for hc in (2 * i, 2 * i + 1) for off in (0, FFN)])
            g["gu%d_%d" % (l, i)] = (_grp(w_gate_up[l][:, cols]), 8, 512)
        for j in range(8):
            g["wd%d_%d" % (l, j)] = (_grp(w_down[l][:, j * 128:(j + 1) * 128]), NHC, 128)
    return g


GROUP_SHAPES = {}
for _h in range(6):
    GROUP_SHAPES["a%d" % _h] = (8, 512)
    GROUP_SHAPES["b%d" % _h] = (8, 384)
GROUP_SHAPES["amq"] = (8, 256)
GROUP_SHAPES["bmq"] = (8, 256)
for _n in ("kv0", "kv1", "kv2"):
    GROUP_SHAPES[_n] = (8, 512)
for _l in range(2):
    GROUP_SHAPES["mkv%d" % _l] = (8, 512)
    GROUP_SHAPES["wo%d_0" % _l] = (8, 512)
    GROUP_SHAPES["wo%d_1" % _l] = (8, 512)
    for _i in range(NHC // 2):
        GROUP_SHAPES["gu%d_%d" % (_l, _i)] = (8, 512)
    for _j in range(8):
        GROUP_SHAPES["wd%d_%d" % (_l, _j)] = (NHC, 128)


def host_consts():
    c = {}
    c["ones_d"] = np.full((128, 128), 1.0 / D, ml_dtypes.bfloat16)
    c["ones_h"] = np.full((128, 128), 1.0 / 128, ml_dtypes.bfloat16)
    o64 = np.zeros((128, 128), np.float32)
    o64[:64, :64] = 1.0 / 64
    o64[64:, 64:] = 1.0 / 64
    c["ones_64"] = o64.astype(ml_dtypes.bfloat16)
    c["ones_f"] = np.ones((128, 128), ml_dtypes.bfloat16)
    bl = np.zeros((128, 2, 128), np.float32)
    bl[:, 0, :64] = 1.0
    bl[:, 1, 64:] = 1.0
    c["blk_ones"] = bl.astype(ml_dtypes.bfloat16)
    c["ident"] = np.eye(128, dtype=np.float32).astype(ml_dtypes.bfloat16)
    R = np.zeros((128, 128), np.float32)
    for do in range(64):
        R[do + 64, do] = -1.0
    for do in range(64, 128):
        R[do - 64, do] = 1.0
    c["rrot"] = R.astype(ml_dtypes.bfloat16)
    kk = np.arange(128)[:, None]
    qq = np.arange(128)[None, :]
    cm = ((kk <= qq) & (kk // 64 == qq // 64)).astype(np.float32)
    c["cmask"] = np.ascontiguousarray(np.broadcast_to(cm[:, None, :], (128, 4, 128))).astype(ml_dtypes.bfloat16)
    rm = np.ones((128, T), np.float32)
    rm[:, ::64] = 0.0
    c["rmask"] = rm
    for dil in DILS:
        W = 128 * dil + 128
        x = np.arange(W)[None, :]
        dd = x - kk
        m = ((dd >= 0) & (dd <= 128 * dil) & (dd % dil == 0)).astype(np.float32)
        c["dmask%d" % dil] = m.astype(ml_dtypes.bfloat16)
    half = 64
    inv = (10000.0 ** (-np.arange(half, dtype=np.float32) / half)).astype(np.float32)
    ang = np.arange(S, dtype=np.float32)[:, None] * inv[None, :]
    cos = np.cos(ang).astype(np.float32).T
    sin = np.sin(ang).astype(np.float32).T
    c["cosT"] = np.ascontiguousarray(np.concatenate([cos, cos], 0))
    c["sinT"] = np.ascontiguousarray(np.concatenate([sin, sin], 0))
    return c


CONST_SHAPES = {
    "ones_d": ([128, 128], BF16), "ones_h": ([128, 128], BF16), "ones_64": ([128, 128], BF16),
    "ones_f": ([128, 128], BF16), "blk_ones": ([128, 2, 128], BF16), "ident": ([128, 128], BF16),
    "rrot": ([128, 128], BF16), "cmask": ([128, 4, 128], BF16), "rmask": ([128, T], F32),
    "dmask1": ([128, 256], BF16), "dmask4": ([128, 640], BF16), "dmask16": ([128, 2176], BF16),
}

PC = {}
_o = 0
for _name, _n in (("norm_mix", 16), ("norm_ffn", 16), ("kv_norm", 8), ("mem_norm", 16), ("a_onorm", 6),
                  ("b_qnorm", 3), ("b_knorm", 1), ("mem_qnorm", 2), ("mem_knorm", 2), ("lb0", 6), ("lb1", 6)):
    PC[_name] = _o
    _o += _n
NPC = _o


def host_params(norm_mix, norm_ffn, a_lb_logits, a_onorm, b_qnorm, kv_norm, b_knorm, mem_norm,
                mem_qnorm, mem_knorm):
    p = np.zeros((128, NPC), np.float32)

    def colmaj(v):
        return v.reshape(-1, 128).T

    for l in range(2):
        p[:, PC["norm_mix"] + 8 * l:PC["norm_mix"] + 8 * l + 8] = colmaj(norm_mix[l])
        p[:, PC["norm_ffn"] + 8 * l:PC["norm_ffn"] + 8 * l + 8] = colmaj(norm_ffn[l])
        p[:, PC["mem_norm"] + 8 * l:PC["mem_norm"] + 8 * l + 8] = colmaj(mem_norm[l])
        p[:, PC["mem_qnorm"] + l] = np.tile(mem_qnorm[l], 2)
        p[:, PC["mem_knorm"] + l] = np.tile(mem_knorm[l], 2)
    p[:, PC["kv_norm"]:PC["kv_norm"] + 8] = colmaj(kv_norm)
    p[:, PC["a_onorm"]:PC["a_onorm"] + 6] = colmaj(a_onorm[0])
    p[:, PC["b_qnorm"]:PC["b_qnorm"] + 3] = b_qnorm[0].T
    p[:, PC["b_knorm"]] = b_knorm
    p[:, PC["lb0"]:PC["lb0"] + 6] = colmaj(a_lb_logits[0])
    p[:, PC["lb1"]:PC["lb1"] + 6] = colmaj(a_lb_logits[1])
    return p


def build(ntiles=NT, nlayers=2):
    nc = bass.Bass("TRN2", target_bir_lowering=False)
    P = Prog(nc)

    xT = P.dram("xT", [D, S], F32, kind="ExternalInput")
    memT = P.dram("memT", [D, NMEM], F32, kind="ExternalInput")
    params = P.dram("params", [128, NPC], F32, kind="ExternalInput")
    cosT = P.dram("cosT", [128, S], F32, kind="ExternalInput")
    sinT = P.dram("sinT", [128, S], F32, kind="ExternalInput")
    outT = P.dram("outT", [D, S], F32, kind="ExternalOutput")
    cdram = {n: P.dram("c_" + n, sh, dt, kind="ExternalInput") for n, (sh, dt) in CONST_SHAPES.items()}
    wf = {}
    wb = {}
    for n, (kc, ncol) in GROUP_SHAPES.items():
        wf[n] = P.dram("wf_" + n, [128, kc * ncol], F32, kind="ExternalInput")
        if not n.startswith("mkv"):
            wb[n] = P.dram("wb_" + n, [128, kc * ncol], BF16)

    csb = {n: P.sbuf("k_" + n, sh, dt) for n, (sh, dt) in CONST_SHAPES.items()}
    par = P.sbuf("par", [128, NPC], F32)
    lbv = P.sbuf("lbv", [128, 12], F32)
    hT = P.sbuf("hT", [128, 8, T], F32)
    xn = P.sbuf("xn", [128, 8, T], BF16)
    mix = P.sbuf("mix", [128, 8, T], BF16)
    hTc = [Tile(hT.h, "hTc%d" % c) for c in range(8)]
    xnc = [Tile(xn.h, "xnc%d" % c) for c in range(8)]
    outc = [Tile(None, "outc%d" % c) for c in range(8)]
    P.tiles.extend(hTc + xnc + outc)
    CH = {id(hT): hTc, id(xn): xnc}

    def chunk_tile(t, c):
        lst = CH.get(id(t))
        return lst[c] if lst is not None else t
    hffn = P.sbuf("hffn", [128, NHC, T], BF16)
    wbuf = [P.sbuf("wbuf%d" % i, [128, 4096], BF16) for i in range(NWB)]
    KT = [P.sbuf("KT%d" % s, [128, 6, T], BF16) for s in range(NSLOT)]
    VV = [P.sbuf("VV%d" % s, [128, 4, 768], BF16) for s in range(NSLOT)]
    cos_t = P.sbuf("cos_t", [128, T], F32)
    sin_t = P.sbuf("sin_t", [128, T], F32)
    mkT = [P.sbuf("mkT%d" % l, [128, 2, NMEM], BF16) for l in range(2)]
    mva = [P.sbuf("mva%d" % l, [128, 2, 4, 128], BF16) for l in range(2)]
    Sst = [P.sbuf("S%d" % h, [128, 128], F32) for h in range(6)]
    Sbf = [[P.sbuf("Sbf%d_%d" % (h, i), [128, 128], BF16) for i in range(2)] for h in range(6)]
    NB2 = 2
    qin = [P.sbuf("qin%d" % i, [128, T], BF16) for i in range(NB2)]
    kin = [P.sbuf("kin%d" % i, [128, T], BF16) for i in range(NB2)]
    koT = [P.sbuf("koT%d" % i, [128, T], BF16) for i in range(NB2)]
    kotm = [P.sbuf("kotm%d" % i, [128, 4, 128], BF16) for i in range(NB2)]
    itm = [P.sbuf("itm%d" % i, [128, 4, 128], BF16) for i in range(NB2)]
    gT = [P.sbuf("gT%d" % i, [128, T], BF16) for i in range(NB2)]
    attT = [P.sbuf("attT%d" % i, [128, 4, 128], BF16) for i in range(NB2)]
    dech = [P.sbuf("dec%d" % i, [128, 8], F32) for i in range(NB2)]
    fA = [P.sbuf("fA%d" % i, [128, T], F32) for i in range(NB2)]
    fB = [P.sbuf("fB%d" % i, [128, T], F32) for i in range(NB2)]
    fC = [P.sbuf("fC%d" % i, [128, T], F32) for i in range(NB2)]
    sqb = [P.sbuf("sqb%d" % i, [128, T], BF16) for i in range(NB2)]
    knb = [P.sbuf("knb%d" % i, [128, T], BF16) for i in range(NB2)]
    qT = [[P.sbuf("qT%d_%d" % (i, g), [128, T], BF16) for g in range(3)] for i in range(NB2)]
    NPT = 9
    pT = [P.sbuf("pT%d" % i, [128, T], BF16) for i in range(5)]
    NX = 4
    pTx = [Tile(hffn.h[:, NHC - NX + i, :], "pTx%d" % i) for i in range(NX)]
    P.tiles.extend(pTx)
    CH[id(hffn)] = [hffn] * (NHC - NX) + pTx
    pT.extend(pTx)
    rstd = fB[0]

    ps = [P.psum("ps%d" % i, [128, 512], F32) for i in range(8)]
    rr_state = {"ps": 0, "w": 0, "pt": 0}
    loaded = set()

    def next_ps():
        i = rr_state["ps"]
        rr_state["ps"] = (i + 1) % 5
        return ps[i]

    def next_ps3():
        i = rr_state["ps"] % 3
        rr_state["ps"] = (i + 1) % 3
        return ps[i]

    def next_pt():
        i = rr_state["pt"]
        rr_state["pt"] = (i + 1) % NPT
        return pT[i]

    def pcol(name, i=0):
        o = PC[name] + i
        return par[:, o:o + 1]

    def load_w(name):
        kc, ncol = GROUP_SHAPES[name]
        n = kc * ncol
        i = rr_state["w"]
        rr_state["w"] = (i + 1) % NWB
        buf = wbuf[i]
        if name not in loaded:
            loaded.add(name)
            P.dma("pool", buf[:, 0:n], wf[name][:], reads=[wf[name]], writes=[buf])
            if name in wb:
                P.dma("sp", wb[name][:], buf[:, 0:n], reads=[buf], writes=[wb[name]])
        else:
            P.dma("sp", buf[:, 0:n], wb[name][:], reads=[wb[name]], writes=[buf])
        view = buf[:, 0:n].rearrange("p (c n) -> p c n", c=kc)
        return buf, view

    def rstd_from_mean(ps_mean, n, out_t):
        P.add("act", lambda e: e.activation(out=out_t[:, 0:n], in_=ps_mean[:, 0:n], func=AF.Ln, bias=EPS),
              reads=[ps_mean], writes=[out_t])
        P.add("act", lambda e: e.activation(out=out_t[:, 0:n], in_=out_t[:, 0:n], func=AF.Exp, scale=-0.5),
              reads=[out_t], writes=[out_t])

    def norm_full(src, gain_name, gi, dst, n):
        for c in range(8):
            P.add("act", lambda e: e.activation(out=dst[:, c, 0:n], in_=src[:, c, 0:n], func=AF.Square),
                  reads=[chunk_tile(src, c)], writes=[chunk_tile(dst, c)])
        pm = next_ps()
        for c in range(8):
            P.add("pe", lambda e: e.matmul(pm[:, 0:n], lhsT=csb["ones_d"][:], rhs=dst[:, c, 0:n],
                                           start=(c == 0), stop=(c == 7)),
                  reads=[csb["ones_d"], chunk_tile(dst, c)], writes=[pm])
        rstd_from_mean(pm, n, rstd)
        for c in range(8):
            P.add("dve", lambda e: e.scalar_tensor_tensor(out=dst[:, c, 0:n], in0=src[:, c, 0:n],
                                                         scalar=pcol(gain_name, gi * 8 + c),
                                                         in1=rstd[:, 0:n], op0=ALU.mult, op1=ALU.mult),
                  reads=[chunk_tile(src, c), par, rstd], writes=[chunk_tile(dst, c)])

    def head_norm(psrc, n, ones_name, gain_ap, dst_ap, dst_tile, slot):
        sq = sqb[slot]
        P.add("act", lambda e: e.activation(out=sq[:, 0:n], in_=psrc[:, 0:n], func=AF.Square),
              reads=[psrc], writes=[sq])
        pm = next_ps()
        P.add("pe", lambda e: e.matmul(pm[:, 0:n], lhsT=csb[ones_name][:], rhs=sq[:, 0:n], start=True, stop=True),
              reads=[csb[ones_name], sq], writes=[pm])
        r = fB[slot]
        rstd_from_mean(pm, n, r)
        P.add("dve", lambda e: e.scalar_tensor_tensor(out=dst_ap, in0=psrc[:, 0:n], scalar=gain_ap,
                                                     in1=r[:, 0:n], op0=ALU.mult, op1=ALU.mult),
              reads=[psrc, par, r], writes=[dst_tile])

    def rope(src_bf, dst_ap, dst_tile, slot):
        pr = next_ps()
        P.add("pe", lambda e: e.matmul(pr[:], lhsT=csb["rrot"][:], rhs=src_bf[:], start=True, stop=True),
              reads=[csb["rrot"], src_bf], writes=[pr])
        t1 = fC[slot]
        t2 = fA[slot]
        P.add("dve", lambda e: e.tensor_tensor(out=t1[:], in0=src_bf[:], in1=cos_t[:], op=ALU.mult),
              reads=[src_bf, cos_t], writes=[t1])
        P.add("dve", lambda e: e.tensor_tensor(out=t2[:], in0=pr[:], in1=sin_t[:], op=ALU.mult),
              reads=[pr, sin_t], writes=[t2])
        P.add("dve", lambda e: e.tensor_tensor(out=dst_ap, in0=t1[:], in1=t2[:], op=ALU.add),
              reads=[t1, t2], writes=[dst_tile])

    def head_norm_g(psrc, n, ones_name, gain_ap, dst_ap, dst_tile, slot):
        sq = sqb[slot]
        P.add("act", lambda e: e.activation(out=sq[:, 0:n], in_=psrc[:, 0:n], func=AF.Square),
              reads=[psrc], writes=[sq])
        yield
        pm = next_ps()
        P.add("pe", lambda e: e.matmul(pm[:, 0:n], lhsT=csb[ones_name][:], rhs=sq[:, 0:n], start=True, stop=True),
              reads=[csb[ones_name], sq], writes=[pm])
        r = fB[slot]
        rstd_from_mean(pm, n, r)
        P.add("dve", lambda e: e.scalar_tensor_tensor(out=dst_ap, in0=psrc[:, 0:n], scalar=gain_ap,
                                                     in1=r[:, 0:n], op0=ALU.mult, op1=ALU.mult),
              reads=[psrc, par, r], writes=[dst_tile])
        t1 = fC[slot]
        yield

    def rope_g(src_bf, dst_ap, dst_tile, slot):
        t1 = fC[slot]
        t2 = fA[slot]
        P.add("dve", lambda e: e.tensor_tensor(out=t1[:], in0=src_bf[:], in1=cos_t[:], op=ALU.mult),
              reads=[src_bf, cos_t], writes=[t1])
        yield
        pr = next_ps()
        P.add("pe", lambda e: e.matmul(pr[:], lhsT=csb["rrot"][:], rhs=src_bf[:], start=True, stop=True),
              reads=[csb["rrot"], src_bf], writes=[pr])
        P.add("dve", lambda e: e.tensor_tensor(out=t2[:], in0=pr[:], in1=sin_t[:], op=ALU.mult),
              reads=[pr, sin_t], writes=[t2])
        P.add("dve", lambda e: e.tensor_tensor(out=dst_ap, in0=t1[:], in1=t2[:], op=ALU.add),
              reads=[t1, t2], writes=[dst_tile])
        yield

    def qk_prep(psrc, gain_ap, dst_ap, dst_tile, slot):
        xg = knb[slot]
        sq = sqb[slot]
        P.add("act", lambda e: e.activation(out=xg[:], in_=psrc[:], func=AF.Copy, scale=gain_ap),
              reads=[psrc, par], writes=[xg])
        P.add("act", lambda e: e.activation(out=sq[:], in_=psrc[:], func=AF.Square), reads=[psrc], writes=[sq])
        yield
        pm = next_ps()
        P.add("pe", lambda e: e.matmul(pm[:], lhsT=csb["ones_h"][:], rhs=sq[:], start=True, stop=True),
              reads=[csb["ones_h"], sq], writes=[pm])
        pr = next_ps()
        P.add("pe", lambda e: e.matmul(pr[:], lhsT=csb["rrot"][:], rhs=xg[:], start=True, stop=True),
              reads=[csb["rrot"], xg], writes=[pr])
        r = fB[slot]
        rstd_from_mean(pm, T, r)
        t1 = fC[slot]
        t2 = fA[slot]
        P.add("dve", lambda e: e.tensor_tensor(out=t1[:], in0=xg[:], in1=cos_t[:], op=ALU.mult),
              reads=[xg, cos_t], writes=[t1])
        yield
        P.add("dve", lambda e: e.tensor_tensor(out=t2[:], in0=pr[:], in1=sin_t[:], op=ALU.mult),
              reads=[pr, sin_t], writes=[t2])
        P.add("dve", lambda e: e.tensor_tensor(out=t1[:], in0=t1[:], in1=t2[:], op=ALU.add),
              reads=[t1, t2], writes=[t1])
        P.add("dve", lambda e: e.tensor_tensor(out=dst_ap, in0=t1[:], in1=r[:], op=ALU.mult),
              reads=[t1, r], writes=[dst_tile])
        yield

    def lin_fm(wtile, wview, col0, rhs_tile, kc, out_ps, n, c0=0, c1=None):
        for c in range(c0, kc if c1 is None else c1):
            P.add("pe", lambda e: e.matmul(out_ps[:, 0:n], lhsT=wview[:, c, col0:col0 + 128], rhs=rhs_tile[:, c, 0:n],
                                           start=(c == 0), stop=(c == kc - 1)),
                  reads=[wtile, chunk_tile(rhs_tile, c)], writes=[out_ps])

    def interleave(*gens):
        gens = [g for g in gens if g is not None]
        while gens:
            for g in list(gens):
                try:
                    next(g)
                except StopIteration:
                    gens.remove(g)

    for n in CONST_SHAPES:
        P.dma("sp", csb[n][:], cdram[n][:], reads=[cdram[n]], writes=[csb[n]])
    P.dma("sp", par[:], params[:], reads=[params], writes=[par])
    P.dma("sp", hT[:, :, 0:NMEM], memT.h.rearrange("(c p) t -> p c t", p=128), reads=[memT], writes=hTc,
          sem_tile=hTc[0])

    P.add("dve", lambda e: e.tensor_tensor(out=lbv[:, 0:6], in0=par[:, PC["lb0"]:PC["lb0"] + 6],
                                          in1=par[:, PC["lb1"]:PC["lb1"] + 6], op=ALU.subtract),
          reads=[par], writes=[lbv])
    P.add("act", lambda e: e.activation(out=lbv[:, 0:6], in_=lbv[:, 0:6], func=AF.Sigmoid),
          reads=[lbv], writes=[lbv])
    P.add("dve", lambda e: e.tensor_scalar(out=lbv[:, 6:12], in0=lbv[:, 0:6], scalar1=-1.0, scalar2=1.0,
                                          op0=ALU.mult, op1=ALU.add),
          reads=[lbv], writes=[lbv])
    for h in range(6):
        P.add("dve", lambda e: e.memset(Sst[h][:], 0.0), writes=[Sst[h]])
        P.add("dve", lambda e: e.memset(Sbf[h][0][:], 0.0), writes=[Sbf[h][0]])

    for l in range(nlayers):
        norm_full(hT, "mem_norm", l, xn, NMEM)
        wt, wv = load_w("mkv%d" % l)
        P.add("dve", lambda e: e.memset(mva[l][:], 0.0), writes=[mva[l]])
        for j in range(2):
            pk = next_ps()
            lin_fm(wt, wv, j * 128, xn, 8, pk, NMEM)
            head_norm(pk, NMEM, "ones_64", pcol("mem_knorm", l), mkT[l][:, j, :], mkT[l], 0)
        for s in range(2):
            pv = next_ps()
            for c in range(8):
                P.add("pe", lambda e: e.matmul(pv[:, 0:256], lhsT=xn[:, c, s * 128:(s + 1) * 128],
                                               rhs=wv[:, c, 256:512], start=(c == 0), stop=(c == 7)),
                      reads=[wt, xnc[c]], writes=[pv])
            for hm in range(4):
                o = (hm % 2) * 64
                P.add("act", lambda e: e.activation(out=mva[l][:, s, hm, o:o + 64], in_=pv[:, hm * 64:(hm + 1) * 64],
                                                    func=AF.Copy),
                      reads=[pv], writes=[mva[l]])

    def memory_attention(l, wname):
        wt, wv = load_w(wname)
        banks = [ps[7], ps[4]] if l == 0 else [ps[7]]
        st = {"i": 0}

        def mtrans():
            b = banks[st["i"] % len(banks)]
            st["i"] += 1
            return b

        xg = qT[1][0]
        mqn = knb[1]
        sq = sqb[1]
        r = fB[1]
        rec = fC[1]
        pnum = ps[5]
        pden = ps[6]
        for j in range(2):
            pq = mtrans()
            lin_fm(wt, wv, j * 128, xn, 8, pq, T)
            P.add("act", lambda e: e.activation(out=xg[:], in_=pq[:], func=AF.Copy, scale=pcol("mem_qnorm", l)),
                  reads=[pq, par], writes=[xg])
            P.add("act", lambda e: e.activation(out=sq[:], in_=pq[:], func=AF.Square), reads=[pq], writes=[sq])
            yield
            pm = mtrans()
            P.add("pe", lambda e: e.matmul(pm[:], lhsT=csb["ones_64"][:], rhs=sq[:], start=True, stop=True),
                  reads=[csb["ones_64"], sq], writes=[pm])
            rstd_from_mean(pm, T, r)
            P.add("dve", lambda e: e.tensor_tensor(out=mqn[:], in0=xg[:], in1=r[:], op=ALU.mult),
                  reads=[xg, r], writes=[mqn])
            yield
            pend = []
            items = [(hm, s) for hm in (2 * j, 2 * j + 1) for s in range(2)]
            cnt = {"n": 0}

            def flush(it, last):
                hm, s, pt = it
                first = cnt["n"] == 0
                cnt["n"] += 1
                P.add("pe", lambda e: e.matmul(pnum[:], lhsT=mva[l][:, s, hm, :], rhs=pt[:],
                                               start=first, stop=last, skip_group_check=True),
                      reads=[mva[l], pt], writes=[pnum])
                P.add("pe", lambda e: e.matmul(pden[:], lhsT=csb["blk_ones"][:, hm % 2, :], rhs=pt[:],
                                               start=first, stop=last, skip_group_check=True),
                      reads=[csb["blk_ones"], pt], writes=[pden])

            for (hm, s) in items:
                o = (hm % 2) * 64
                pscore = mtrans()
                P.add("pe", lambda e: e.matmul(pscore[:], lhsT=mkT[l][o:o + 64, j, s * 128:(s + 1) * 128],
                                               rhs=mqn[o:o + 64, :], start=True, stop=True),
                      reads=[mkT[l], mqn], writes=[pscore])
                pt = next_pt()
                P.add("act", lambda e: e.activation(out=pt[:], in_=pscore[:], func=AF.Exp, scale=0.125),
                      reads=[pscore], writes=[pt])
                pend.append((hm, s, pt))
                if len(pend) > 2:
                    flush(pend.pop(0), False)
                yield
            while pend:
                flush(pend.pop(0), len(pend) == 0)
            P.add("act", lambda e: e.activation(out=rec[:], in_=pden[:], func=AF.Ln), reads=[pden], writes=[rec])
            P.add("act", lambda e: e.activation(out=rec[:], in_=rec[:], func=AF.Exp, scale=-1.0),
                  reads=[rec], writes=[rec])
            P.add("dve", lambda e: e.tensor_tensor(out=mix[:, 6 + j, :], in0=pnum[:], in1=rec[:], op=ALU.mult),
                  reads=[pnum, rec], writes=[mix])
            yield

    def out_proj_and_ffn(l, store_p0=None):
        for half in range(2):
            wt, wv = load_w("wo%d_%d" % (l, half))
            for jj in range(4):
                j = half * 4 + jj
                po = next_ps()
                lin_fm(wt, wv, jj * 128, mix, 8, po, T)
                P.add("dve", lambda e: e.tensor_tensor(out=hT[:, j, :], in0=po[:], in1=hT[:, j, :], op=ALU.add),
                      reads=[po, hTc[j]], writes=[hTc[j]])
        norm_full(hT, "norm_ffn", l, xn, T)
        for i in range(NHC // 2):
            wt, wv = load_w("gu%d_%d" % (l, i))
            for k in range(2):
                hc = 2 * i + k
                pg = next_ps()
                pu = next_ps()
                lin_fm(wt, wv, k * 256, xn, 8, pg, T)
                lin_fm(wt, wv, k * 256 + 128, xn, 8, pu, T)
                sg = fA[hc % 2]
                P.add("act", lambda e: e.activation(out=sg[:], in_=pg[:], func=AF.Silu), reads=[pg], writes=[sg])
                P.add("dve", lambda e: e.tensor_tensor(out=hffn[:, hc, :], in0=pu[:], in1=sg[:], op=ALU.mult),
                      reads=[pu, sg], writes=[chunk_tile(hffn, hc)])
        for j in range(8):
            wt, wv = load_w("wd%d_%d" % (l, j))
            po = next_ps()
            lin_fm(wt, wv, 0, hffn, NHC, po, T)
            P.add("dve", lambda e: e.tensor_tensor(out=hT[:, j, :], in0=po[:], in1=hT[:, j, :], op=ALU.add),
                  reads=[po, hTc[j]], writes=[hTc[j]])
            if store_p0 is not None:
                P.dma("pool", outT[j * 128:(j + 1) * 128, store_p0:store_p0 + T], hT[:, j, :], reads=[hTc[j]],
                      writes=[outc[j]], sem_tile=outc[j])

    pdS = {}

    def hgrn_A(h):
        sl = h % 2
        wt, wv = load_w("a%d" % h)
        a, b_, c_ = fA[sl], fB[sl], fC[sl]
        pi = ps[3]

        def pi_mm(s):
            for c in range(8):
                P.add("pe", lambda e: e.matmul(pi[:, s * 128:(s + 1) * 128], lhsT=xn[:, c, s * 128:(s + 1) * 128],
                                               rhs=wv[:, c, 256:384], start=(c == 0), stop=(c == 7)),
                      reads=[wt, xnc[c]], writes=[pi])

        pf = next_ps3()
        lin_fm(wt, wv, 128, xn, 8, pf, T, 0, 4)
        yield
        lin_fm(wt, wv, 128, xn, 8, pf, T, 4, 8)
        P.add("act", lambda e: e.activation(out=a[:], in_=pf[:], func=AF.Sigmoid), reads=[pf], writes=[a])
        yield
        pq = next_ps3()
        lin_fm(wt, wv, 0, xn, 8, pq, T, 0, 4)
        yield
        lin_fm(wt, wv, 0, xn, 8, pq, T, 4, 8)
        P.add("dve", lambda e: e.tensor_scalar(out=a[:], in0=a[:], scalar1=lbv[:, 6 + h:7 + h],
                                              scalar2=lbv[:, h:h + 1], op0=ALU.mult, op1=ALU.add),
              reads=[a, lbv], writes=[a])
        P.add("act", lambda e: e.activation(out=b_[:], in_=a[:], func=AF.Ln), reads=[a], writes=[b_])
        yield
        pi_mm(0)
        P.add("dve", lambda e: e.tensor_tensor_scan(out=c_[:], data0=csb["rmask"][:], data1=b_[:],
                                                   initial=0.0, op0=ALU.mult, op1=ALU.add),
              reads=[csb["rmask"], b_], writes=[c_])
        P.add("dve", lambda e: e.tensor_scalar(out=a[:], in0=a[:], scalar1=-1.0, scalar2=1.0,
                                              op0=ALU.mult, op1=ALU.add),
              reads=[a], writes=[a])
        P.add("act", lambda e: e.activation(out=b_[:], in_=c_[:], func=AF.Exp), reads=[c_], writes=[b_])
        P.add("act", lambda e: e.activation(out=c_[:], in_=c_[:], func=AF.Exp, scale=-1.0),
              reads=[c_], writes=[c_])
        yield
        pi_mm(1)
        P.add("dve", lambda e: e.tensor_copy(out=dech[sl][:], in_=b_[:].rearrange("p (c k) -> p c k", k=64)[:, :, 63]),
              reads=[b_], writes=[dech[sl]])
        P.add("dve", lambda e: e.tensor_tensor(out=kin[sl][:], in0=a[:], in1=c_[:], op=ALU.mult),
              reads=[a, c_], writes=[kin[sl]])
        P.add("act", lambda e: e.activation(out=a[:], in_=pq[:], func=AF.Silu), reads=[pq], writes=[a])
        yield
        pi_mm(2)
        for cc in range(8):
            P.add("dve", lambda e: e.tensor_scalar(out=koT[sl][:, cc * 64:(cc + 1) * 64],
                                                  in0=kin[sl][:, cc * 64:(cc + 1) * 64],
                                                  scalar1=dech[sl][:, cc:cc + 1], scalar2=None, op0=ALU.mult),
                  reads=[kin[sl], dech[sl]], writes=[koT[sl]])
        P.add("dve", lambda e: e.tensor_tensor(out=qin[sl][:], in0=a[:], in1=b_[:], op=ALU.mult),
              reads=[a, b_], writes=[qin[sl]])
        yield
        pi_mm(3)
        P.add("act", lambda e: e.activation(out=itm[sl][:].rearrange("p s n -> p (s n)"), in_=pi[:], func=AF.Copy),
              reads=[pi], writes=[itm[sl]])
        yield
        trt = next_ps3()
        trv = trt.h.bitcast(BF16)
        for s in range(4):
            P.add("pe", lambda e: e.transpose(trv[:, s * 128:(s + 1) * 128], koT[sl][:, s * 128:(s + 1) * 128],
                                              csb["ident"][:]),
                  reads=[koT[sl], csb["ident"]], writes=[trt])
        P.add("act", lambda e: e.activation(out=kotm[sl][:].rearrange("p s n -> p (s n)"), in_=trv[:, 0:512],
                                            func=AF.Copy),
              reads=[trt], writes=[kotm[sl]])
        yield
        pa = next_ps3()
        for s in range(4):
            P.add("pe", lambda e: e.matmul(pa[:, s * 128:(s + 1) * 128], lhsT=kin[sl][:, s * 128:(s + 1) * 128],
                                           rhs=qin[sl][:, s * 128:(s + 1) * 128], start=True, stop=True),
                  reads=[kin[sl], qin[sl]], writes=[pa])
        P.add("dve", lambda e: e.tensor_tensor(out=attT[sl][:].rearrange("p s n -> p (s n)"), in0=pa[:],
                                              in1=csb["cmask"][:].rearrange("p s n -> p (s n)"), op=ALU.mult),
              reads=[pa, csb["cmask"]], writes=[attT[sl]])
        yield
        pg = next_ps3()
        lin_fm(wt, wv, 384, xn, 8, pg, T)
        P.add("act", lambda e: e.activation(out=gT[sl][:], in_=pg[:], func=AF.Silu), reads=[pg], writes=[gT[sl]])
        yield

    def hgrn_B(h):
        sl = h % 2
        po = ps[6 + (h % 2)]
        pd = [ps[4], ps[5]]
        for cc in range(8):
            s = cc // 2
            o = (cc % 2) * 64
            P.add("pe", lambda e: e.matmul(pd[cc % 2][:, (cc // 2) * 128:(cc // 2 + 1) * 128],
                                           lhsT=kotm[sl][o:o + 64, s, :], rhs=itm[sl][o:o + 64, s, :],
                                           start=True, stop=True),
                  reads=[kotm[sl], itm[sl]], writes=[pd[cc % 2]])
        yield
        for cc in range(8):
            s = cc // 2
            o = (cc % 2) * 64
            cs_ = slice(cc * 64, (cc + 1) * 64)
            cur = Sbf[h][cc % 2]
            nxt = Sbf[h][(cc + 1) % 2]
            P.add("pe", lambda e: e.matmul(po[:, cs_], lhsT=cur[:], rhs=qin[sl][:, cs_],
                                           start=True, stop=False, skip_group_check=True),
                  reads=[cur, qin[sl]], writes=[po])
            P.add("pe", lambda e: e.matmul(po[:, cs_], lhsT=itm[sl][o:o + 64, s, :], rhs=attT[sl][o:o + 64, s, o:o + 64],
                                           start=False, stop=True, skip_group_check=True),
                  reads=[itm[sl], attT[sl]], writes=[po])
            pdc = pd[cc % 2][:, (cc // 2) * 128:(cc // 2 + 1) * 128]
            P.add("dve", lambda e: e.scalar_tensor_tensor(out=nxt[:], in0=Sst[h][:], scalar=dech[sl][:, cc:cc + 1],
                                                         in1=pdc, op0=ALU.mult, op1=ALU.add),
                  reads=[Sst[h], dech[sl], pd[cc % 2]], writes=[nxt])
            P.add("dve", lambda e: e.scalar_tensor_tensor(out=Sst[h][:], in0=Sst[h][:], scalar=dech[sl][:, cc:cc + 1],
                                                         in1=pdc, op0=ALU.mult, op1=ALU.add),
                  reads=[Sst[h], dech[sl], pd[cc % 2]], writes=[Sst[h]])
            yield
        sq = sqb[sl]
        P.add("act", lambda e: e.activation(out=sq[:], in_=po[:], func=AF.Square), reads=[po], writes=[sq])
        pm = next_ps3()
        P.add("pe", lambda e: e.matmul(pm[:], lhsT=csb["ones_h"][:], rhs=sq[:], start=True, stop=True),
              reads=[csb["ones_h"], sq], writes=[pm])
        r = fB[sl]
        rstd_from_mean(pm, T, r)
        a = fA[sl]
        P.add("dve", lambda e: e.scalar_tensor_tensor(out=a[:], in0=po[:], scalar=pcol("a_onorm", h), in1=r[:],
                                                     op0=ALU.mult, op1=ALU.mult),
              reads=[po, par, r], writes=[a])
        P.add("dve", lambda e: e.tensor_tensor(out=mix[:, h, :], in0=a[:], in1=gT[sl][:], op=ALU.mult),
              reads=[a, gT[sl]], writes=[mix])
        yield

    def attn_A(t, h):
        sl = h % 2
        wt, wv = load_w("b%d" % h)
        for g in range(3):
            pq = next_ps()
            lin_fm(wt, wv, g * 128, xn, 8, pq, T)
            for _ in qk_prep(pq, pcol("b_qnorm", g), qT[sl][g][:], qT[sl][g], sl):
                yield

    def attn_B(t, h, LOOK=7):
        sl = h % 2
        p0 = t * T
        pnum = ps[5] if h % 2 == 0 else ps[7]
        pden = ps[6]
        blocks = []
        for g, dil in enumerate(DILS):
            for kb in range(max(0, 4 * t - dil), 4 * t + 4):
                kj = 128 * kb
                qlo = max(0, kj - p0)
                qhi = min(T, kj - p0 + 128 * dil + 128)
                if qhi <= qlo:
                    continue
                blocks.append((g, dil, kb, qlo, qhi, p0 + qlo - kj))
        pend = []
        cnt = {"n": 0}

        def flush(it, last):
            g, dil, kb, qlo, qhi, x0, pt = it
            first = cnt["n"] == 0
            cnt["n"] += 1
            ks = (kb // 4) % NSLOT
            sub = kb % 4
            P.add("pe", lambda e: e.matmul(pnum[:, qlo:qhi], lhsT=VV[ks][:, sub, h * 128:(h + 1) * 128],
                                           rhs=pt[:, qlo:qhi], start=first, stop=last, skip_group_check=True),
                  reads=[VV[ks], pt], writes=[pnum])
            P.add("pe", lambda e: e.matmul(pden[:, qlo:qhi], lhsT=csb["ones_f"][:], rhs=pt[:, qlo:qhi],
                                           start=first, stop=last, skip_group_check=True),
                  reads=[csb["ones_f"], pt], writes=[pden])

        for bi, (g, dil, kb, qlo, qhi, x0) in enumerate(blocks):
            ks = (kb // 4) % NSLOT
            sub = kb % 4
            pscore = next_ps()
            P.add("pe", lambda e: e.matmul(pscore[:, qlo:qhi], lhsT=KT[ks][:, h, sub * 128:(sub + 1) * 128],
                                           rhs=qT[sl][g][:, qlo:qhi], start=True, stop=True),
                  reads=[KT[ks], qT[sl][g]], writes=[pscore])
            pt = next_pt()
            P.add("act", lambda e: e.activation(out=pt[:, qlo:qhi], in_=pscore[:, qlo:qhi], func=AF.Exp,
                                                scale=float(128 ** -0.5)),
                  reads=[pscore], writes=[pt])
            mk = csb["dmask%d" % dil]
            P.add("dve", lambda e: e.tensor_tensor(out=pt[:, qlo:qhi], in0=pt[:, qlo:qhi],
                                                  in1=mk[:, x0:x0 + (qhi - qlo)], op=ALU.mult),
                  reads=[pt, mk], writes=[pt])
            pend.append((g, dil, kb, qlo, qhi, x0, pt))
            if len(pend) > LOOK:
                flush(pend.pop(0), False)
            yield
        while pend:
            flush(pend.pop(0), len(pend) == 0)
        rec = fC[sl]
        P.add("act", lambda e: e.activation(out=rec[:], in_=pden[:], func=AF.Ln), reads=[pden], writes=[rec])
        P.add("act", lambda e: e.activation(out=rec[:], in_=rec[:], func=AF.Exp, scale=-1.0),
              reads=[rec], writes=[rec])
        P.add("dve", lambda e: e.tensor_tensor(out=mix[:, h, :], in0=pnum[:], in1=rec[:], op=ALU.mult),
              reads=[pnum, rec], writes=[mix])
        yield

    for t in range(ntiles):
        p0 = t * T
        slot = t % NSLOT
        for c in range(8):
            P.dma("sp", hT[:, c, :], xT[c * 128:(c + 1) * 128, p0:p0 + T], reads=[xT], writes=[hTc[c]])
        P.dma("sp", cos_t[:], cosT[:, p0:p0 + T], reads=[cosT], writes=[cos_t])
        P.dma("sp", sin_t[:], sinT[:, p0:p0 + T], reads=[sinT], writes=[sin_t])

        norm_full(hT, "norm_mix", 0, xn, T)
        interleave(memory_attention(0, "amq"), hgrn_A(0))
        for h in range(6):
            interleave(hgrn_B(h), hgrn_A(h + 1) if h < 5 else None)
        out_proj_and_ffn(0, p0 if nlayers == 1 else None)

        if nlayers > 1:
            norm_full(hT, "kv_norm", 0, xn, T)
            wt0, wv0 = load_w("kv0")
            wt1, wv1 = load_w("kv1")
            wt2, wv2 = load_w("kv2")

            def kgen():
                for h in range(6):
                    sl = h % 2
                    wt, wv = (wt0, wv0) if h < 4 else (wt1, wv1)
                    pk = next_ps()
                    lin_fm(wt, wv, (h % 4) * 128, xn, 8, pk, T)
                    for _ in qk_prep(pk, pcol("b_knorm"), KT[slot][:, h, :], KT[slot], sl):
                        yield

            def vgen():
                for s in range(4):
                    pv = next_ps()
                    for c in range(8):
                        P.add("pe", lambda e: e.matmul(pv[:, 0:256], lhsT=xn[:, c, s * 128:(s + 1) * 128],
                                                       rhs=wv1[:, c, 256:512], start=(c == 0), stop=(c == 7)),
                              reads=[wt1, xnc[c]], writes=[pv])
                    P.add("act", lambda e: e.activation(out=VV[slot][:, s, 0:256], in_=pv[:, 0:256], func=AF.Copy),
                          reads=[pv], writes=[VV[slot]])
                    yield
                    pv2 = next_ps()
                    for c in range(8):
                        P.add("pe", lambda e: e.matmul(pv2[:], lhsT=xn[:, c, s * 128:(s + 1) * 128],
                                                       rhs=wv2[:, c, :], start=(c == 0), stop=(c == 7)),
                              reads=[wt2, xnc[c]], writes=[pv2])
                    P.add("act", lambda e: e.activation(out=VV[slot][:, s, 256:768], in_=pv2[:], func=AF.Copy),
                          reads=[pv2], writes=[VV[slot]])
                    yield

            interleave(kgen(), vgen())

            norm_full(hT, "norm_mix", 1, xn, T)
            interleave(memory_attention(1, "bmq"), attn_A(t, 0))
            nblk = sum(1 for dil in DILS for kb in range(max(0, 4 * t - dil), 4 * t + 4))
            kk = max(1, min(4, nblk // 9))
            for h in range(6):
                gb = attn_B(t, h)
                ga = attn_A(t, h + 1) if h < 5 else None
                i = 0
                for _ in gb:
                    i += 1
                    if ga is not None and i % kk == 0:
                        try:
                            next(ga)
                        except StopIteration:
                            ga = None
                if ga is not None:
                    for _ in ga:
                        pass
            out_proj_and_ffn(1, p0)


    P.emit(final_wait_tiles=outc)
    P.close()
    return nc


_CACHE = {}


def kernel(x, mem, norm_mix, norm_ffn, a_w_in, a_lb_logits, a_onorm, b_w_in, b_qnorm, kv_norm, w_kv,
           b_knorm, mem_norm, w_mem_kv, mem_qnorm, mem_knorm, w_out, w_gate_up, w_down, _ntiles=NT, _nlayers=2):
    f = lambda a: np.asarray(a, dtype=np.float32)
    x, mem = f(x), f(mem)
    groups = weight_groups(f(a_w_in), f(b_w_in), f(w_kv), f(w_mem_kv), f(w_out), f(w_gate_up), f(w_down))
    consts = host_consts()
    params = host_params(f(norm_mix), f(norm_ffn), f(a_lb_logits), f(a_onorm), f(b_qnorm), f(kv_norm),
                         f(b_knorm), f(mem_norm), f(mem_qnorm), f(mem_knorm))
    key = (_ntiles, _nlayers)
    if key not in _CACHE:
        _CACHE[key] = build(_ntiles, _nlayers)
    nc = _CACHE[key]
    B = x.shape[0]
    shared = {"params": params, "cosT": consts["cosT"], "sinT": consts["sinT"]}
    for n in CONST_SHAPES:
        shared["c_" + n] = consts[n]
    for n, (arr, kc, ncol) in groups.items():
        shared["wf_" + n] = arr
    in_maps = []
    for b in range(B):
        m = dict(shared)
        m["xT"] = np.ascontiguousarray(x[b].T)
        m["memT"] = np.ascontiguousarray(mem[b].T)
        in_maps.append(m)
    res = run_bass_kernel_spmd(nc, in_maps, core_ids=list(range(B)))
    out = np.stack([np.ascontiguousarray(r["outT"].T) for r in res.results], axis=0)
    return out.astype(np.float32)
```
